# Optimizing a Trainium2 kernel written in Bass

```python
import jax, jax.numpy as jnp
from jax import lax
import numpy as np

D_MODEL = 1024
BATCH = 8
SEQ = 2048
DEPTH = 4

D_CONV = D_MODEL
CONV_K = 3
N_HEADS = 16
QK_NOPE = 64
QK_ROPE = 32
V_HEAD = 64
Q_LORA = 768
KV_LORA = 256
ROPE_THETA = 10000.0
Q_BLOCK = 128
SOFTMAX_SCALE = (QK_NOPE + QK_ROPE) ** -0.5
D_FF = 4 * D_MODEL
LN_EPS = 1e-5
RMS_EPS = 1e-6
DEEPNORM_ALPHA = (2 * DEPTH) ** 0.25
DEEPNORM_BETA = (8 * DEPTH) ** -0.25

OFF_B = D_CONV
OFF_C = 2 * D_CONV
OFF_H = 3 * D_CONV
OFF_QA = OFF_H + Q_LORA
OFF_KVA = OFF_QA + KV_LORA
OFF_KR = OFF_KVA + QK_ROPE
IN_PROJ_WIDTH = OFF_KR + 2 * D_MODEL
SPLIT_POINTS = (OFF_B, OFF_C, OFF_H, OFF_QA, OFF_KVA, OFF_KR)

kernel_name = "hybrid_conv_mla_deepnorm_encoder"


def layer_norm(x, g, b):
    xf = x.astype(jnp.float32)
    mu = jnp.mean(xf, axis=-1, keepdims=True)
    xc = xf - mu
    var = jnp.mean(xc * xc, axis=-1, keepdims=True)
    return (xc * lax.rsqrt(var + LN_EPS) * g + b).astype(x.dtype)


def rms_norm(x, g):
    xf = x.astype(jnp.float32)
    ms = jnp.mean(xf * xf, axis=-1, keepdims=True)
    return (xf * lax.rsqrt(ms + RMS_EPS) * g).astype(x.dtype)


def rope_cos_sin(positions, dtype):
    inv_freq = 1.0 / (ROPE_THETA ** (jnp.arange(0, QK_ROPE, 2, dtype=jnp.float32) / QK_ROPE))
    ang = positions.astype(jnp.float32)[..., None] * inv_freq
    return jnp.cos(ang).astype(dtype), jnp.sin(ang).astype(dtype)


def apply_rope(x, cos, sin):
    half = QK_ROPE // 2
    x1, x2 = x[..., :half], x[..., half:]
    return jnp.concatenate([x1 * cos - x2 * sin, x1 * sin + x2 * cos], axis=-1)


def short_conv(u, w):
    up = jnp.pad(u, ((0, 0), (1, 1), (0, 0)))
    return w[0] * up[:, :-2] + w[1] * up[:, 1:-1] + w[2] * up[:, 2:]


def mla_attention(q_nope, q_rope, k_nope, k_rope, v):
    b, s = q_nope.shape[0], q_nope.shape[1]
    nb = s // Q_BLOCK
    qn = q_nope.reshape(b, nb, Q_BLOCK, N_HEADS, QK_NOPE).transpose(1, 0, 2, 3, 4)
    qr = q_rope.reshape(b, nb, Q_BLOCK, N_HEADS, QK_ROPE).transpose(1, 0, 2, 3, 4)

    def one_block(args):
        qn_b, qr_b = args
        scores = (jnp.einsum('bqhd,bkhd->bhqk', qn_b, k_nope)
                  + jnp.einsum('bqhr,bkr->bhqk', qr_b, k_rope))
        p = jax.nn.softmax(scores.astype(jnp.float32) * SOFTMAX_SCALE, axis=-1).astype(v.dtype)
        return jnp.einsum('bhqk,bkhd->bqhd', p, v)

    out = lax.map(one_block, (qn, qr))
    return out.transpose(1, 0, 2, 3, 4).reshape(b, s, N_HEADS * V_HEAD)


def token_mixing(x, cos, sin, w_in, b_gate, conv_w, w_conv_out, q_norm_g, w_q_b,
                 kv_norm_g, w_kv_b, w_mla_o, w_out):
    b, s, _ = x.shape
    proj = x @ w_in
    bg, cg, hc, q_a, kv_a, k_r, gate_logits = jnp.split(proj, SPLIT_POINTS, axis=-1)
    y_conv = (bg * short_conv(cg * hc, conv_w)) @ w_conv_out
    q = (rms_norm(q_a, q_norm_g) @ w_q_b).reshape(b, s, N_HEADS, QK_NOPE + QK_ROPE)
    q_nope = q[..., :QK_NOPE]
    q_rope = apply_rope(q[..., QK_NOPE:], cos[:, :, None, :], sin[:, :, None, :])
    kv = (rms_norm(kv_a, kv_norm_g) @ w_kv_b).reshape(b, s, N_HEADS, QK_NOPE + V_HEAD)
    k_nope, v = kv[..., :QK_NOPE], kv[..., QK_NOPE:]
    k_rope = apply_rope(k_r, cos, sin)
    y_mla = mla_attention(q_nope, q_rope, k_nope, k_rope, v) @ w_mla_o
    g = jax.nn.sigmoid(gate_logits + b_gate)
    g_conv, g_mla = g[..., :D_MODEL], g[..., D_MODEL:]
    return (g_conv * y_conv + g_mla * y_mla) @ w_out


def sq_relu_mlp(x, w_up, w_down):
    h = jax.nn.relu(x @ w_up)
    return (h * h) @ w_down


def setup_inputs(seed: int = 0) -> dict:
    key = jax.random.key(seed)
    ks = jax.random.split(key, 24)
    f32 = jnp.float32

    def nrm(k, shape, scale):
        return jax.random.normal(k, shape, f32) * scale

    def gain(k, shape):
        return 1.0 + 0.02 * jax.random.normal(k, shape, f32)

    L = DEPTH
    x = jax.random.normal(ks[0], (BATCH, SEQ, D_MODEL), f32)
    offsets = jax.random.randint(ks[1], (BATCH, 1), 0, SEQ, dtype=jnp.int32)
    positions = offsets + jnp.arange(SEQ, dtype=jnp.int32)[None, :]
    return {
        "x": x,
        "positions": positions,
        "ln_in_g": gain(ks[2], (D_MODEL,)),
        "ln_in_b": nrm(ks[3], (D_MODEL,), 0.02),
        "w_in": nrm(ks[4], (L, D_MODEL, IN_PROJ_WIDTH), D_MODEL ** -0.5),
        "b_gate": nrm(ks[5], (L, 2 * D_MODEL), 0.02),
        "conv_w": nrm(ks[6], (L, CONV_K, D_CONV), CONV_K ** -0.5),
        "w_conv_out": nrm(ks[7], (L, D_CONV, D_MODEL), D_CONV ** -0.5),
        "q_norm_g": gain(ks[8], (L, Q_LORA)),
        "w_q_b": nrm(ks[9], (L, Q_LORA, N_HEADS * (QK_NOPE + QK_ROPE)), Q_LORA ** -0.5),
        "kv_norm_g": gain(ks[10], (L, KV_LORA)),
        "w_kv_b": nrm(ks[11], (L, KV_LORA, N_HEADS * (QK_NOPE + V_HEAD)), KV_LORA ** -0.5),
        "w_mla_o": nrm(ks[12], (L, N_HEADS * V_HEAD, D_MODEL), (N_HEADS * V_HEAD) ** -0.5),
        "w_out": nrm(ks[13], (L, D_MODEL, D_MODEL), DEEPNORM_BETA * D_MODEL ** -0.5),
        "ln_mix_g": gain(ks[14], (L, D_MODEL)),
        "ln_mix_b": nrm(ks[15], (L, D_MODEL), 0.02),
        "w_up": nrm(ks[16], (L, D_MODEL, D_FF), D_MODEL ** -0.5),
        "w_down": nrm(ks[17], (L, D_FF, D_MODEL), DEEPNORM_BETA * D_FF ** -0.5),
        "ln_ffn_g": gain(ks[18], (L, D_MODEL)),
        "ln_ffn_b": nrm(ks[19], (L, D_MODEL), 0.02),
    }


def reference(x, positions, ln_in_g, ln_in_b, w_in, b_gate, conv_w, w_conv_out,
              q_norm_g, w_q_b, kv_norm_g, w_kv_b, w_mla_o, w_out, ln_mix_g, ln_mix_b,
              w_up, w_down, ln_ffn_g, ln_ffn_b):
    cos, sin = rope_cos_sin(positions, x.dtype)
    h = layer_norm(x, ln_in_g, ln_in_b)
    for l in range(DEPTH):
        mix = token_mixing(h, cos, sin, w_in[l], b_gate[l], conv_w[l], w_conv_out[l],
                           q_norm_g[l], w_q_b[l], kv_norm_g[l], w_kv_b[l], w_mla_o[l], w_out[l])
        h = layer_norm(DEEPNORM_ALPHA * h + mix, ln_mix_g[l], ln_mix_b[l])
        h = layer_norm(DEEPNORM_ALPHA * h + sq_relu_mlp(h, w_up[l], w_down[l]), ln_ffn_g[l], ln_ffn_b[l])
    return h
```

```python
import contextlib
import numpy as np
import concourse.bass as bass
import concourse.mybir as mybir
from concourse.bass_utils import run_bass_kernel_spmd

F32 = mybir.dt.float32
BF16 = mybir.dt.bfloat16
I32 = mybir.dt.int32
AF = mybir.ActivationFunctionType
ALU = mybir.AluOpType
DSIZE = {F32: 4, BF16: 2, I32: 4}

ENGS = ("pe", "act", "dve", "pool", "sp")
GRAN = 512
EPOCH = 20000
NDMASEM = 8

S = 2048
D = 1024
NL = 4
ALPHA = float((2 * NL) ** 0.25)
SCALE = float(96 ** -0.5)
LN_EPS = 1e-5
RMS_EPS = 1e-6
OFF_QA = 3072
OFF_KR = 4096
OFF_G = 4128


class Prog:
    def __init__(self, nc):
        self.nc = nc
        self.recs = []
        self.streams = {e: [] for e in ENGS}
        self.lastw = {}
        self.readers = {}
        self.dma_rr = {e: 0 for e in ENGS}
        self.dma_cnt = {}
        self.out_dma = []
        self.stopped = False

    def _grans(self, ap):
        t = ap.tensor
        if "DRam" in type(t).__name__:
            return ()
        dsz = DSIZE[ap.dtype]
        rowb = int(np.prod(list(t.shape)[1:])) * DSIZE[t.dtype]
        rowe = rowb // dsz
        col = int(ap.offset) % rowe
        ext = 1
        for s, c in ap.ap[1:]:
            ext += (c - 1) * abs(s)
        lo = col * dsz
        hi = (col + ext) * dsz
        key = t.name
        return [(key, g) for g in range(lo // GRAN, (hi - 1) // GRAN + 1)]

    def _deps(self, eng, reads, writes, is_dma):
        raw = set()
        war = set()
        rg = []
        wg = []
        for ap in reads:
            rg.extend(self._grans(ap))
        for ap in writes:
            wg.extend(self._grans(ap))
        for g in rg:
            w = self.lastw.get(g)
            if w is not None:
                raw.add(w)
        for g in wg:
            w = self.lastw.get(g)
            if w is not None:
                raw.add(w)
            for r in self.readers.get(g, ()):
                war.add(r)
        gid = len(self.recs)
        for g in rg:
            self.readers.setdefault(g, []).append(gid)
        for g in wg:
            self.lastw[g] = gid
            self.readers[g] = []
        deps = set()
        for d in raw | war:
            r = self.recs[d]
            if r["dma"]:
                deps.add(d)
                continue
            if r["eng"] == eng and not is_dma and eng == "pe":
                continue
            deps.add(d)
        for d in deps:
            self.recs[d]["flag"] = True
        return deps

    def op(self, eng, fn, reads=(), writes=()):
        if self.stopped:
            return -1
        deps = self._deps(eng, reads, writes, False)
        gid = len(self.recs)
        rec = dict(eng=eng, fn=fn, deps=deps, dma=False, flag=False, gid=gid)
        self.recs.append(rec)
        self.streams[eng].append(rec)
        return gid

    def dma(self, eng, out, in_, is_output=False, **kw):
        if self.stopped:
            return -1
        deps = self._deps(eng, [in_], [out], True)
        gid = len(self.recs)
        i = self.dma_rr[eng]
        self.dma_rr[eng] = (i + 1) % NDMASEM
        sname = f"{eng}_dma{i}"
        prev = self.dma_cnt.get(sname, 0)
        self.dma_cnt[sname] = prev + 16
        rec = dict(eng=eng, fn=lambda e: e.dma_start(out=out, in_=in_, **kw), deps=deps, dma=True,
                   flag=True, gid=gid, dsem=sname, dval=prev + 16, dprev=prev)
        self.recs.append(rec)
        self.streams[eng].append(rec)
        if is_output:
            self.out_dma.append(gid)
        return gid

    def emit(self):
        nc = self.nc
        for e in ENGS:
            k = 0
            for r in self.streams[e]:
                if r["flag"] and not r["dma"]:
                    k += 1
                    r["ord"] = k
        semnames = set()
        for e in ENGS:
            n = sum(1 for r in self.streams[e] if r["flag"] and not r["dma"])
            for ep in range(n // EPOCH + 1):
                semnames.add(f"{e}_c{ep}")
        for s in self.dma_cnt:
            semnames.add(s)
        semnames = sorted(semnames)
        with contextlib.ExitStack() as st:
            sems = {s: st.enter_context(nc.semaphore(s)) for s in semnames}
            block = st.enter_context(nc.Block())

            def token_wait(d):
                r = self.recs[d]
                if r["dma"]:
                    return (r["dsem"], r["dval"], None)
                o = r["ord"] - 1
                return (f"{r['eng']}_c{o // EPOCH}", o % EPOCH + 1, (r["eng"], o))

            def run_stream(e, engine):
                waited_eng = {}
                waited_dma = {}
                for r in self.streams[e]:
                    best = {}
                    for d in r["deps"]:
                        sname, val, eo = token_wait(d)
                        if eo is None:
                            if waited_dma.get(sname, 0) >= val:
                                continue
                            waited_dma[sname] = val
                        else:
                            pe_, o = eo
                            if waited_eng.get(pe_, -1) >= o:
                                continue
                            waited_eng[pe_] = o
                        best[sname] = max(best.get(sname, 0), val)
                    if r["dma"] and r["dprev"] > 0:
                        if waited_dma.get(r["dsem"], 0) < r["dprev"]:
                            waited_dma[r["dsem"]] = r["dprev"]
                            best[r["dsem"]] = max(best.get(r["dsem"], 0), r["dprev"])
                    for sname, val in best.items():
                        engine.wait_ge(sems[sname], val)
                    inst = r["fn"](engine)
                    if r["dma"]:
                        inst.then_inc(sems[r["dsem"]], 16)
                    elif r["flag"]:
                        o = r["ord"] - 1
                        inst.then_inc(sems[f"{e}_c{o // EPOCH}"], 1)
                if e == "sp":
                    for d in self.out_dma:
                        rr = self.recs[d]
                        engine.wait_ge(sems[rr["dsem"]], rr["dval"])

            @block.tensor
            def _(eng):
                run_stream("pe", eng)

            @block.scalar
            def _(eng):
                run_stream("act", eng)

            @block.vector
            def _(eng):
                run_stream("dve", eng)

            @block.gpsimd
            def _(eng):
                run_stream("pool", eng)

            @block.sync
            def _(eng):
                run_stream("sp", eng)


class Cut(Exception):
    pass


def build(nl=NL, cut=None, dump=None):
    nc = bass.Bass("TRN2", target_bir_lowering=False)
    Dm = {}

    def din(name, shape, dt=F32):
        Dm[name] = nc.dram_tensor(name, shape, dt, kind="ExternalInput").ap()

    din("x", [S, D])
    din("pos", [128, 16], I32)
    din("invf", [128, 16])
    din("ident", [128, 128])
    din("ln_in_g", [1, D])
    din("ln_in_b", [1, D])
    din("w_in", [NL, D, 6176])
    din("b_gate", [128, NL, 16])
    din("conv_w", [128, NL, 3, 8])
    din("w_conv_out", [NL, D, D])
    din("q_norm_g", [128, NL, 6])
    din("w_q_b", [NL, 768, 1536])
    din("kv_norm_g", [128, NL, 2])
    din("w_kv_b", [NL, 256, 2048])
    din("w_mla_o", [NL, D, D])
    din("w_out", [NL, D, D])
    din("ln_mix_g", [NL, D])
    din("ln_mix_b", [NL, D])
    din("w_up", [NL, D, 4 * D])
    din("w_down", [NL, 4 * D, D])
    din("ln_ffn_g", [NL, D])
    din("ln_ffn_b", [NL, D])
    out = nc.dram_tensor("out", [S, D], F32, kind="ExternalOutput").ap()

    TOTAL = 211968
    with contextlib.ExitStack() as st:
        ar = st.enter_context(nc.sbuf_tensor("arena", [128, TOTAL // 2], BF16))
        ps = st.enter_context(nc.psum_tensor("ps", [128, 8, 512], F32))
        P = Prog(nc)

        def ck(name):
            if cut == name:
                P.stopped = True

        def V(off, nbytes, dt=BF16, pat=None, **kw):
            assert off % 4 == 0
            a = ar[:, off // 2:(off + nbytes) // 2]
            if dt != BF16:
                a = a.bitcast(dt)
            if pat is not None:
                a = a.rearrange(pat, **kw)
            return a

        O_HI, O_LO, O_HT, O_A, O_B, O_WS, O_M = 0, 32768, 65536, 98304, 131072, 163840, 196608
        HI = V(O_HI, 32768, BF16, "p (a n) -> p a n", a=16)
        LO = V(O_LO, 32768, BF16, "p (a n) -> p a n", a=16)
        HT = V(O_HT, 32768, BF16, "p (a n) -> p a n", a=8)
        WS = [V(O_WS + i * 8192, 8192, BF16, "p (a n) -> p a n", a=8) for i in range(4)]
        IDB = V(O_M + 0, 256)
        ONESB = V(O_M + 512, 256)
        ONES32 = V(O_M + 1024, 512, F32)
        COS = V(O_M + 1536, 1024, F32, "p (a n) -> p a n", a=16)
        SIN = V(O_M + 2560, 1024, F32, "p (a n) -> p a n", a=16)
        COSQ = V(O_M + 3584, 1024, F32, "p (a n) -> p a n", a=16)
        SINQ = V(O_M + 4608, 1024, F32, "p (a n) -> p a n", a=16)
        BGATE = V(O_M + 5632, 256, F32, "p (a n) -> p a n", a=NL)
        CONVW = V(O_M + 5888, 384, F32, "p (l k c) -> p l k c", l=NL, k=3)
        QG = V(O_M + 6272, 96, F32, "p (a n) -> p a n", a=NL)
        KVG = V(O_M + 6368, 32, F32, "p (a n) -> p a n", a=NL)
        O_ST = O_M + 6656
        O_SC = O_M + 7680
        assert O_SC + 7680 == TOTAL

        def bank(b, n=512):
            return ps[:, b, 0:n]

        def bankbf(b):
            return ps[:, b, :].bitcast(BF16)

        def mm(out, lhsT, rhs, start, stop):
            P.op("pe", lambda e: e.matmul(out, lhsT=lhsT, rhs=rhs, start=start, stop=stop),
                 reads=[lhsT, rhs], writes=[out])

        def tr(out, in_):
            np_ = in_.shape[0]
            idn = IDB[0:np_, 0:np_]
            P.op("pe", lambda e: e.transpose(out=out, in_=in_, identity=idn), reads=[in_, idn], writes=[out])

        def tt(eng, out, in0, in1, op):
            P.op(eng, lambda e: e.tensor_tensor(out=out, in0=in0, in1=in1, op=op), reads=[in0, in1], writes=[out])

        def ts(eng, out, in0, s1, s2, op0, op1=None):
            rd = [in0] + [s for s in (s1, s2) if not isinstance(s, (int, float, type(None)))]
            if op1 is None:
                P.op(eng, lambda e: e.tensor_single_scalar(out=out, in_=in0, scalar=s1, op=op0), reads=rd, writes=[out])
            else:
                P.op(eng, lambda e: e.tensor_scalar(out=out, in0=in0, scalar1=s1, scalar2=s2, op0=op0, op1=op1),
                     reads=rd, writes=[out])

        def stt(eng, out, in0, sc, in1, op0, op1):
            rd = [in0, in1] + ([sc] if not isinstance(sc, (int, float)) else [])
            P.op(eng, lambda e: e.scalar_tensor_tensor(out=out, in0=in0, scalar=sc, in1=in1, op0=op0, op1=op1),
                 reads=rd, writes=[out])

        def cp(eng, out, in_):
            if eng == "act":
                P.op("act", lambda e: e.copy(out=out, in_=in_), reads=[in_], writes=[out])
            else:
                P.op(eng, lambda e: e.tensor_copy(out=out, in_=in_), reads=[in_], writes=[out])

        def act(out, in_, func, bias=None, scale=None):
            kw = {}
            rd = [in_]
            if bias is not None:
                kw["bias"] = bias
                if not isinstance(bias, (int, float)):
                    rd.append(bias)
            if scale is not None:
                kw["scale"] = scale
            P.op("act", lambda e: e.activation(out=out, in_=in_, func=func, **kw), reads=rd, writes=[out])

        def memset(eng, ap, val):
            P.op(eng, lambda e: e.memset(ap, val), reads=[], writes=[ap])

        evac_rr = [0]

        def evac(out, in_):
            evac_rr[0] ^= 1
            cp("act" if evac_rr[0] else "dve", out, in_)

        def wview(name, l):
            return Dm[name][l].rearrange("(kc p) n -> p kc n", p=128)

        def wload(slot, src):
            P.dma("pool", slot, src)

        tmp32 = V(O_SC, 512, F32)
        P.dma("sp", tmp32, Dm["ident"])
        cp("dve", IDB, tmp32)
        memset("dve", ONESB, 1.0)
        memset("dve", ONES32, 1.0)
        P.dma("sp", BGATE, Dm["b_gate"])
        P.dma("sp", CONVW, Dm["conv_w"])
        P.dma("sp", QG, Dm["q_norm_g"])
        P.dma("sp", KVG, Dm["kv_norm_g"])
        posi = V(O_SC + 512, 64, I32)
        posf = V(O_SC + 1024, 64, F32)
        invf = V(O_SC + 1536, 64, F32)
        ang = V(O_SC + 2048, 1024, F32, "p (a n) -> p a n", a=16)
        arg = V(O_SC + 3072, 1024, F32, "p (a n) -> p a n", a=16)
        P.dma("sp", posi, Dm["pos"])
        P.dma("sp", invf, Dm["invf"])
        cp("dve", posf, posi)
        for t in range(16):
            ts("dve", ang[:, t, :], invf, posf[:, t:t + 1], None, ALU.mult)
        PI = float(np.pi)
        kf = V(O_SC + 4096, 1024, F32, "p (a n) -> p a n", a=16)
        ki = V(O_SC + 5120, 1024, I32, "p (a n) -> p a n", a=16)
        msk = V(O_SC + 6144, 1024, F32, "p (a n) -> p a n", a=16)
        for tab, shift in ((SIN, 0.0), (COS, 0.5 * PI)):
            ts("dve", arg, ang, shift, None, ALU.add)
            ts("dve", kf, arg, 1.0 / (2 * PI), None, ALU.mult)
            cp("dve", ki, kf)
            cp("dve", kf, ki)
            stt("dve", arg, kf, -2 * PI, arg, ALU.mult, ALU.add)
            ts("dve", msk, arg, PI, None, ALU.is_gt)
            stt("dve", arg, msk, -2 * PI, arg, ALU.mult, ALU.add)
            ts("dve", msk, arg, -PI, None, ALU.is_lt)
            stt("dve", arg, msk, 2 * PI, arg, ALU.mult, ALU.add)
            ts("dve", arg, arg, -PI, PI, ALU.max, ALU.min)
            act(tab, arg, AF.Sin)
        ts("dve", COSQ, COS, SCALE, None, ALU.mult)
        ts("dve", SINQ, SIN, SCALE, None, ALU.mult)

        ck("setup")

        def bc_heads(tab, t0):
            v = tab[:, t0:t0 + 2, :]
            a = [list(x) for x in v.ap]
            return bass.AP(v.tensor, v.offset, [a[0], a[1], [0, 2], a[2]])

        def ln_stage1(R, t, eps=LN_EPS):
            so = O_ST + (t % 2) * 512
            bn = V(so, 48, F32, "p (a n) -> p a n", a=2)
            mv = V(so + 64, 8, F32)
            rstd = V(so + 96, 4, F32)
            P.op("dve", lambda e: e.bn_stats(out=bn[:, 0, :], in_=R[:, 0:512]), reads=[R[:, 0:512]], writes=[bn[:, 0, :]])
            P.op("dve", lambda e: e.bn_stats(out=bn[:, 1, :], in_=R[:, 512:1024]), reads=[R[:, 512:1024]], writes=[bn[:, 1, :]])
            P.op("dve", lambda e: e.bn_aggr(out=mv, in_=bn), reads=[bn], writes=[mv])
            ts("dve", rstd, mv[:, 1:2], eps, None, ALU.add)
            act(rstd, rstd, AF.Sqrt)

        def ln_stage2(Rin, R, G, B, t, final):
            so = O_ST + (t % 2) * 512
            mv = V(so + 64, 8, F32)
            rstd = V(so + 96, 4, F32)
            P.op("dve", lambda e: e.reciprocal(out=rstd, in_=rstd), reads=[rstd], writes=[rstd])
            stt("dve", R, Rin, mv[:, 0:1], G, ALU.subtract, ALU.mult)
            stt("dve", R, R, rstd, B, ALU.mult, ALU.add)
            if final:
                P.dma("sp", out[t * 128:(t + 1) * 128, :], R, is_output=True)
            else:
                cp("act", HI[:, t, :], R)
                tt("pool", LO[:, t, :], R, HI[:, t, :], ALU.subtract)

        def ln_pipeline(prep, G, B, final, eps=LN_EPS, hook=True):
            Rs = {}
            for t in range(17):
                if t < 16:
                    r = prep(t)
                    Rs[t] = r if isinstance(r, tuple) else (r, r)
                    ln_stage1(Rs[t][0], t, eps)
                if t >= 1:
                    ln_stage2(Rs[t - 1][0], Rs[t - 1][1], G, B, t - 1, final)
                    if hook:
                        ln_loop_hook(t - 1)

        def load_gb(off, gname, bname, l):
            G = V(off, 4096, F32)
            B = V(off + 4096, 4096, F32)
            gs = Dm[gname][l:l + 1, :] if l is not None else Dm[gname]
            bs = Dm[bname][l:l + 1, :] if l is not None else Dm[bname]
            P.dma("sp", G.unsqueeze(1), gs.partition_broadcast(128))
            P.dma("sp", B.unsqueeze(1), bs.partition_broadcast(128))
            return G, B

        hb = [0]

        hT_banks = [(0, 1, 2, 3)]

        def build_hT_block(tq):
            for kc in range(8):
                bl = hT_banks[0]
                b = bl[hb[0] % len(bl)]
                hb[0] += 1
                pt = bankbf(b)[:, 0:512].rearrange("p (a n) -> p a n", a=4)
                for j in range(4):
                    tr(pt[:, j, :], HI[:, tq * 4 + j, kc * 128:(kc + 1) * 128])
                evac(HT[:, kc, tq * 512:(tq + 1) * 512], bankbf(b)[:, 0:512])

        def build_hT():
            for tq in range(4):
                build_hT_block(tq)

        def ln_loop_hook(t):
            if t >= 6 and (t - 3) % 4 == 3:
                build_hT_block((t - 3) // 4)
            if t == 15:
                build_hT_block(3)

        G0, B0 = load_gb(O_B, "ln_in_g", "ln_in_b", None)
        def prep_entry(t):
            X = V(O_B + 8192 + (t % 4) * 4096, 4096, F32)
            P.dma("sp", X, Dm["x"][t * 128:(t + 1) * 128, :])
            return X

        ln_pipeline(prep_entry, G0, B0, False)
        ck("entryln")
        item = [0]

        def next_slot():
            s = WS[item[0] % 4]
            item[0] += 1
            return s

        for l in range(nl):
            w_in = wview("w_in", l)
            ck("hT")
            base = item[0] % 4
            item[0] += 4 + (0 if l == 0 else 2)
            sh = 0 if l == 0 else 2
            base_slot = (base + sh) % 4
            S1 = WS[(base + sh) % 4]; S2 = WS[(base + sh + 1) % 4]
            A0 = WS[(base + sh + 2) % 4]; Bg0 = WS[(base + sh + 3) % 4]
            wload(S1, w_in[:, :, OFF_QA:OFF_QA + 512])
            wload(S2, w_in[:, :, OFF_QA + 512:OFF_QA + 1024])
            KRW = V(O_SC, 512, BF16, "p (a n) -> p a n", a=8)
            wload(KRW, w_in[:, :, OFF_KR:OFF_KR + 32])
            wload(A0, wview("w_mla_o", l)[:, :, 0:512])
            wload(Bg0, w_in[:, :, OFF_G + 1024:OFF_G + 1536])
            ck("wl")
            QKVN = V(O_A, 32768, BF16, "p (a n) -> p a n", a=8)
            RAW = V(O_B, 16384, F32, "p (a n) -> p a n", a=8)
            SQ = V(O_B + 16384, 8192, BF16, "p (a n) -> p a n", a=8)
            RSTD = V(O_B + 24576, 4096, F32, "p (a n) -> p a n", a=2)
            KRTM = V(O_B + 28672, 3072, BF16, "p (a n) -> p a n", a=16)
            memset("dve", KRTM, 0.0)
            bi = 0
            for tq in range(4):
                tsl = slice(tq * 512, (tq + 1) * 512)
                for j in range(8):
                    b = bi % 4
                    bi += 1
                    Sx = S1 if j < 4 else S2
                    jl = j % 4
                    for kc in range(8):
                        mm(bank(b), Sx[:, kc, jl * 128:(jl + 1) * 128], HT[:, kc, tsl], kc == 0, kc == 7)
                    ck("p1m")
                    cp("dve", RAW[:, j, :], bank(b))
                    ck("p1c")
                    tt("dve", SQ[:, j, :], RAW[:, j, :], RAW[:, j, :], ALU.mult)
                ck("p1j")
                for gi, js, nf in ((0, list(range(6)), 768.0), (1, [6, 7], 256.0)):
                    pb = bank(4 + gi)
                    for idx, j in enumerate(js):
                        mm(pb, ONESB, SQ[:, j, :], idx == 0, idx == len(js) - 1)
                    ts("dve", RSTD[:, gi, :], pb, 1.0 / nf, RMS_EPS, ALU.mult, ALU.add)
                    act(RSTD[:, gi, :], RSTD[:, gi, :], AF.Sqrt)
                    P.op("dve", lambda e, r_=RSTD[:, gi, :]: e.reciprocal(out=r_, in_=r_), reads=[RSTD[:, gi, :]], writes=[RSTD[:, gi, :]])
                ck("p1s")
                for j in range(8):
                    gi = 0 if j < 6 else 1
                    gcol = QG[:, l, j:j + 1] if j < 6 else KVG[:, l, j - 6:j - 5]
                    stt("dve", QKVN[:, j, tsl], RAW[:, j, :], gcol, RSTD[:, gi, :], ALU.mult, ALU.mult)
                ck("p1n")
                pk = ps[:, 6, 0:128].rearrange("p (a n) -> p a n", a=4)
                for j in range(4):
                    t = tq * 4 + j
                    for kc in range(8):
                        mm(pk[:, j, :], HT[:, kc, t * 128:(t + 1) * 128], KRW[:, kc, :], kc == 0, kc == 7)
                ck("p1k")
                x1 = pk[:, :, 0:16]
                x2 = pk[:, :, 16:32]
                c_ = COS[:, tq * 4:tq * 4 + 4, :]
                s_ = SIN[:, tq * 4:tq * 4 + 4, :]
                T = [V(O_SC + 1024 + i * 512, 256, F32, "p (a n) -> p a n", a=4) for i in range(4)]
                tsl4 = slice(tq * 4, tq * 4 + 4)
                tt("dve", T[0], x1, c_, ALU.mult)
                tt("dve", T[1], x2, s_, ALU.mult)
                tt("dve", KRTM[:, tsl4, 64:80], T[0], T[1], ALU.subtract)
                tt("dve", T[2], x1, s_, ALU.mult)
                tt("dve", T[3], x2, c_, ALU.mult)
                tt("dve", KRTM[:, tsl4, 80:96], T[2], T[3], ALU.add)

            ck("p1")
            AT = V(O_B, 32768, BF16, "p (a n) -> p a n", a=8)
            o_t2 = O_WS + base_slot * 8192
            QT = [V(O_HT + i * 8192, 8192, BF16, "p (a n) -> p a n", a=2) for i in range(2)]
            KT = [V(O_HT + 16384 + i * 4096, 4096, BF16) for i in range(2)]
            PT = [V(O_HT + 24576 + i * 1024, 1024, BF16) for i in range(3)]
            OS = V(O_HT + 27648, 2048, F32)
            QTM = [V(O_HT + 29696 + i * 768, 768, BF16, "p (a h d) -> p a h d", a=2, h=2) for i in range(2)]
            RT = [V(O_HT + 31232 + i * 256, 256, F32, "p (a h d) -> p a h d", a=2, h=2) for i in range(4)]
            RB = V(o_t2 + 12288, 1024, BF16)
            OSR = V(o_t2 + 13312, 2048, F32)
            VP = [V(o_t2 + i * 6144, 6144, BF16, "p (a n) -> p a n", a=16) for i in range(2)]
            WQb = [V(O_SC + i * 3584, 2304, BF16, "p (a n) -> p a n", a=6) for i in range(2)]
            WKVb = [V(O_SC + i * 3584 + 2560, 1024, BF16, "p (a n) -> p a n", a=2) for i in range(2)]
            for tq in range(4):
                ptb = bankbf(7)[:, 0:512]
                pt4 = ptb.rearrange("p (a n) -> p a n", a=4)
                for j in range(4):
                    tr(pt4[0:96, j, :], KRTM[:, tq * 4 + j, :])
                cp("dve", KT[0][64:96, tq * 512:(tq + 1) * 512], ptb[64:96, :])
                cp("dve", KT[1][64:96, tq * 512:(tq + 1) * 512], ptb[64:96, :])
            for i in range(2):
                memset("pool", VP[i][:, :, 64:128], 0.0)
                memset("pool", VP[i][:, :, 64:65], 1.0)

            wq_all = Dm["w_q_b"][l].rearrange("(kc p) n -> p kc n", p=128)
            wkv_all = Dm["w_kv_b"][l].rearrange("(kc p) n -> p kc n", p=128)

            prb = [0]

            def prod_bank():
                prb[0] ^= 1
                return 6 + prb[0]

            def load_pair_w(hp):
                wload(WQb[hp % 2], wq_all[:, :, hp * 192:(hp + 1) * 192])
                wload(WKVb[hp % 2], wkv_all[:, :, hp * 256:(hp + 1) * 256])

            def prod_qv(hp):
                par = hp % 2
                WQ, WKV = WQb[par], WKVb[par]
                yield
                for tq in range(4):
                    pT = bankbf(5).rearrange("p (h n) -> p h n", h=2)
                    for half in range(2):
                        t0 = tq * 4 + half * 2
                        bq = prod_bank()
                        for a in range(2):
                            for kc in range(6):
                                mm(ps[:, bq, a * 192:(a + 1) * 192], QKVN[:, kc, (t0 + a) * 128:(t0 + a + 1) * 128],
                                   WQ[:, kc, :], kc == 0, kc == 5)
                        yield
                        pq = ps[:, bq, 0:384].rearrange("p (a h d) -> p a h d", a=2, h=2)
                        qm = QTM[half]
                        ts("dve", qm[:, :, :, 0:64], pq[:, :, :, 0:64], SCALE, None, ALU.mult)
                        x1 = pq[:, :, :, 64:80]
                        x2 = pq[:, :, :, 80:96]
                        cq = bc_heads(COSQ, t0)
                        sq_ = bc_heads(SINQ, t0)
                        tt("dve", RT[0], x1, cq, ALU.mult)
                        tt("dve", RT[1], x2, sq_, ALU.mult)
                        tt("dve", qm[:, :, :, 64:80], RT[0], RT[1], ALU.subtract)
                        tt("dve", RT[2], x1, sq_, ALU.mult)
                        tt("dve", RT[3], x2, cq, ALU.mult)
                        tt("dve", qm[:, :, :, 80:96], RT[2], RT[3], ALU.add)
                        for _ in range(7):
                            yield
                        for a in range(2):
                            for h in range(2):
                                c0 = (half * 2 + a) * 128
                                tr(pT[0:96, h, c0:c0 + 128], qm[:, a, h, :])
                        yield
                    yield
                    cp("dve", QT[par][0:96, :, tq * 512:(tq + 1) * 512], pT[0:96, :, :])
                    yield
                for t4 in range(4):
                    bv = prod_bank()
                    pv = ps[:, bv, :].rearrange("p (a h d) -> p a h d", a=4, h=2)
                    for a in range(4):
                        t = t4 * 4 + a
                        for kc in range(2):
                            rv = WKV[:, kc, :].rearrange("p (h t d) -> p h t d", h=2, t=2)[:, :, 1, :]
                            mm(pv[:, a, :, :], QKVN[:, 6 + kc, t * 128:(t + 1) * 128], rv, kc == 0, kc == 1)
                    yield
                    yield
                    cp("dve", VP[par][:, t4 * 4:t4 * 4 + 4, 0:64], pv[:, :, 0, :])
                    cp("dve", VP[par][:, t4 * 4:t4 * 4 + 4, 128:192], pv[:, :, 1, :])
                    yield

            def prod_k(n):
                hp, hl = n // 2, n % 2
                WKV = WKVb[hp % 2]
                for tq in range(4):
                    bk = prod_bank()
                    pk_ = ps[0:64, bk, :]
                    for kc in range(2):
                        mm(pk_, WKV[:, kc, hl * 128:hl * 128 + 64], QKVN[:, 6 + kc, tq * 512:(tq + 1) * 512],
                           kc == 0, kc == 1)
                    yield
                    yield
                    cp("dve", KT[hl][0:64, tq * 512:(tq + 1) * 512], pk_)
                    yield

            def drain(g):
                if g is not None:
                    for _ in g:
                        pass

            load_pair_w(0)
            load_pair_w(1)
            drain(prod_qv(0))
            drain(prod_k(0))

            steps = [(n, tq, kt) for n in range(16) for tq in range(4) for kt in range(16)]
            NS = len(steps)

            def rec_S(si):
                n, tq, kt = steps[si]
                hp, hl = n // 2, n % 2
                mm(bank(si % 3), KT[hl][0:96, kt * 128:(kt + 1) * 128], QT[hp % 2][0:96, hl, tq * 512:(tq + 1) * 512],
                   True, True)

            gen_qv = None
            gen_k = None
            deferred = []
            rec_S(0)
            rec_S(1)
            oi = 0
            for si in range(NS):
                n, tq, kt = steps[si]
                hp, hl = n // 2, n % 2
                if tq == 0 and kt == 0:
                    drain(gen_k)
                    gen_k = prod_k(n + 1) if n + 1 < 16 else None
                    if hl == 0:
                        drain(gen_qv)
                        gen_qv = prod_qv(hp + 1) if hp + 1 < 8 else None
                    elif hp + 2 < 8:
                        load_pair_w(hp + 2)
                pt_ = PT[si % 3]
                act(pt_, bank(si % 3), AF.Exp)
                if si + 2 < NS:
                    n2 = steps[si + 2][0]
                    if n2 != n:
                        drain(gen_k); gen_k = None
                        if n2 % 2 == 0:
                            drain(gen_qv); gen_qv = None
                    rec_S(si + 2)
                po = ps[:, 3 + (oi % 2), :]
                if hl == 0:
                    mm(po[0:65, :], VP[hp % 2][:, kt, 0:65], pt_, kt == 0, kt == 15)
                else:
                    mm(po[:, :], VP[hp % 2][:, kt, 64:192], pt_, kt == 0, kt == 15)
                for dd in deferred:
                    dd[0] -= 1
                for dd in [d_ for d_ in deferred if d_[0] <= 0]:
                    deferred.remove(dd)
                    dd[1]()
                if kt == 15:
                    tsl = slice(tq * 512, (tq + 1) * 512)
                    if hl == 0:
                        def part1(po=po):
                            cp("dve", OS[0:65, :], po[0:65, :])
                            P.op("dve", lambda e: e.reciprocal(out=OS[64:65, :], in_=OS[64:65, :]), reads=[OS[64:65, :]], writes=[OS[64:65, :]])
                            cp("dve", RB[64:65, :], OS[64:65, :])

                        def part2(hp=hp, tsl=tsl, pbc=po):
                            mm(pbc[0:64, :], ONESB[64:65, 0:64], RB[64:65, :], True, True)
                            tt("dve", AT[0:64, hp, tsl], OS[0:64, :], pbc[0:64, :], ALU.mult)
                    else:
                        def part1(po=po):
                            cp("dve", OSR[0:1, :], po[0:1, :])
                            cp("dve", OS[64:128, :], po[64:128, :])
                            P.op("dve", lambda e: e.reciprocal(out=OSR[0:1, :], in_=OSR[0:1, :]), reads=[OSR[0:1, :]], writes=[OSR[0:1, :]])
                            cp("dve", RB[0:1, :], OSR[0:1, :])

                        def part2(hp=hp, tsl=tsl, pbc=po):
                            mm(pbc[:, :], ONESB[0:1, :], RB[0:1, :], True, True)
                            tt("dve", AT[64:128, hp, tsl], OS[64:128, :], pbc[64:128, :], ALU.mult)
                    deferred.append([2, part1])
                    deferred.append([12, part2])
                    oi += 1
                for g in (gen_qv, gen_k):
                    if g is not None:
                        next(g, None)
            while deferred:
                deferred.pop(0)[1]()

            ck("p2")
            MT = V(O_A, 32768, BF16, "p (a n) -> p a n", a=8)
            GT = [V(O_SC + i * 2048, 2048, F32) for i in range(2)]
            build_hT_block(0)
            A1 = next_slot(); wload(A1, wview("w_mla_o", l)[:, :, 512:1024])
            Bg1 = next_slot(); wload(Bg1, w_in[:, :, OFF_G + 1536:OFF_G + 2048])
            gi_ = 0
            Wc = [None, None]; Wh = [None, None]; Wb = [None, None]
            for cp_ in range(2):
                A_, B_ = (A0, Bg0) if cp_ == 0 else (A1, Bg1)
                if cp_ == 1:
                    Wc[0] = next_slot(); wload(Wc[0], w_in[:, :, 1024:1536])
                    Wh[0] = next_slot(); wload(Wh[0], w_in[:, :, 2048:2560])
                order3 = [(c, tq) for tq in range(4) for c in range(4)] if cp_ == 0 else \
                         [(c, tq) for c in range(4) for tq in range(4)]
                for c, tq in order3:
                    if cp_ == 0 and c == 0 and tq + 1 < 4:
                        build_hT_block(tq + 1)
                    cabs = cp_ * 4 + c
                    tsl = slice(tq * 512, (tq + 1) * 512)
                    py = bank(4 + gi_ % 2)
                    pg = bank(6 + gi_ % 2)
                    g_ = GT[gi_ % 2]
                    gi_ += 1
                    for kc in range(8):
                        mm(py, A_[:, kc, c * 128:(c + 1) * 128], AT[:, kc, tsl], kc == 0, kc == 7)
                    for kc in range(8):
                        mm(pg, B_[:, kc, c * 128:(c + 1) * 128], HT[:, kc, tsl], kc == 0, kc == 7)
                    act(g_, pg, AF.Sigmoid, bias=BGATE[:, l, 8 + cabs:9 + cabs])
                    stt("dve", MT[:, cabs, tsl], g_, 1.0 / ALPHA, py, ALU.mult, ALU.mult)

            ck("p3")
            ZT = V(O_B, 32768, BF16, "p (a n) -> p a n", a=8)
            U = V(O_SC, 4100, BF16)
            TC = V(O_SC + 4608, 2048, F32)
            memset("pool", U[:, 0:2], 0.0)
            memset("pool", U[:, 2048:2050], 0.0)
            Wb[0] = next_slot(); wload(Wb[0], w_in[:, :, 0:512])
            Wc[1] = next_slot(); wload(Wc[1], w_in[:, :, 1536:2048])
            gi_ = 0
            for sc in range(2):
                if sc == 1:
                    Wh[1] = next_slot(); wload(Wh[1], w_in[:, :, 2560:3072])
                    Wb[1] = next_slot(); wload(Wb[1], w_in[:, :, 512:1024])
                for c in range(4):
                    cabs = sc * 4 + c
                    csl = slice(c * 128, (c + 1) * 128)
                    for tq in range(4):
                        tsl = slice(tq * 512, (tq + 1) * 512)
                        pc = bank(gi_ % 2)
                        ph = bank(2 + gi_ % 2)
                        gi_ += 1
                        for kc in range(8):
                            mm(pc, Wc[sc][:, kc, csl], HT[:, kc, tsl], kc == 0, kc == 7)
                        for kc in range(8):
                            mm(ph, Wh[sc][:, kc, csl], HT[:, kc, tsl], kc == 0, kc == 7)
                        usl = U[:, 1 + tq * 512:1 + (tq + 1) * 512]
                        cp("act", usl, pc)
                        tt("dve", usl, usl, ph, ALU.mult)
                    for tq in range(4):
                        tsl = slice(tq * 512, (tq + 1) * 512)
                        pb = bank(4 + tq % 2)
                        for kc in range(8):
                            mm(pb, Wb[sc][:, kc, csl], HT[:, kc, tsl], kc == 0, kc == 7)
                        w0 = CONVW[:, l, 0, cabs:cabs + 1]
                        w1 = CONVW[:, l, 1, cabs:cabs + 1]
                        w2 = CONVW[:, l, 2, cabs:cabs + 1]
                        ts("dve", TC, U[:, tq * 512:tq * 512 + 512], w0, None, ALU.mult)
                        stt("dve", TC, U[:, tq * 512 + 1:tq * 512 + 513], w1, TC, ALU.mult, ALU.add)
                        stt("dve", TC, U[:, tq * 512 + 2:tq * 512 + 514], w2, TC, ALU.mult, ALU.add)
                        tt("dve", ZT[:, cabs, tsl], TC, pb, ALU.mult)

            ck("p4a")
            GT = [V(O_SC + i * 2048, 2048, F32) for i in range(2)]
            T2 = V(O_SC + 4096, 2048, F32)
            Wco = [None, None]; Wg = [None, None]
            Wco[0] = next_slot(); wload(Wco[0], wview("w_conv_out", l)[:, :, 0:512])
            Wg[0] = next_slot(); wload(Wg[0], w_in[:, :, OFF_G:OFF_G + 512])
            gi_ = 0
            for cp_ in range(2):
                if cp_ == 1:
                    Wco[1] = next_slot(); wload(Wco[1], wview("w_conv_out", l)[:, :, 512:1024])
                    Wg[1] = next_slot(); wload(Wg[1], w_in[:, :, OFF_G + 512:OFF_G + 1024])
                for c in range(4):
                    cabs = cp_ * 4 + c
                    csl = slice(c * 128, (c + 1) * 128)
                    for tq in range(4):
                        tsl = slice(tq * 512, (tq + 1) * 512)
                        py = bank(gi_ % 2)
                        pg = bank(2 + gi_ % 2)
                        g_ = GT[gi_ % 2]
                        gi_ += 1
                        for kc in range(8):
                            mm(pg, Wg[cp_][:, kc, csl], HT[:, kc, tsl], kc == 0, kc == 7)
                        for kc in range(8):
                            mm(py, Wco[cp_][:, kc, csl], ZT[:, kc, tsl], kc == 0, kc == 7)
                        act(g_, pg, AF.Sigmoid, bias=BGATE[:, l, cabs:cabs + 1])
                        stt("dve", T2, g_, 1.0 / ALPHA, py, ALU.mult, ALU.mult)
                        tt("dve", MT[:, cabs, tsl], T2, MT[:, cabs, tsl], ALU.add)

            ck("p4b")
            Wo = [next_slot(), next_slot()]
            wload(Wo[0], wview("w_out", l)[:, :, 0:512])
            wload(Wo[1], wview("w_out", l)[:, :, 512:1024])
            Wu = [next_slot(), next_slot()]
            wload(Wu[0], wview("w_up", l)[:, :, 0:512])
            wload(Wu[1], wview("w_up", l)[:, :, 512:1024])
            G, B = load_gb(O_B, "ln_mix_g", "ln_mix_b", l)
            def prep_mix(t):
                b0 = 2 + 2 * (t % 3)
                pm = ps[:, b0:b0 + 2, :].rearrange("p a n -> p (a n)")
                R = V(O_B + 8192 + (t % 4) * 4096, 4096, F32)
                for half in range(2):
                    hs = slice(half * 512, (half + 1) * 512)
                    for kc in range(8):
                        mm(pm[:, hs], MT[:, kc, t * 128:(t + 1) * 128], Wo[half][:, kc, :], kc == 0, False)
                    mm(pm[:, hs], IDB, HI[:, t, hs], False, False)
                    mm(pm[:, hs], IDB, LO[:, t, hs], False, True)
                return (pm, R)

            hT_banks[0] = (0, 1)
            ln_pipeline(prep_mix, G, B, False, eps=LN_EPS / (ALPHA * ALPHA))
            hT_banks[0] = (0, 1, 2, 3)

            ck("p5")
            ACC = V(O_A, 65536, F32, "p (a n) -> p a n", a=16)
            for t in range(16):
                tt("dve", ACC[:, t, :], HI[:, t, :], LO[:, t, :], ALU.add)
            UT = V(O_HI, 32768, BF16, "p (t f n) -> p t f n", t=16, f=8)
            TM = [V(O_SC + i * 2048, 2048, F32) for i in range(2)]
            ui = 0
            wu_off = 0
            for fb in range(4):
                Wd = [next_slot(), next_slot()]
                wd_off = O_WS + ((item[0] - 2) % 4) * 8192
                wd_v = Dm["w_down"][l, fb * 1024:(fb + 1) * 1024, :].rearrange("(fc p) n -> p fc n", p=128)
                wload(Wd[0], wd_v[:, :, 0:512])
                wload(Wd[1], wd_v[:, :, 512:1024])
                order = [(fc, tq) for tq in range(4) for fc in range(8)] if fb == 0 else \
                        [(fc, tq) for fc in range(8) for tq in range(4)]
                for fc, tq in order:
                    Sx = Wu[fc // 4]
                    fl = fc % 4
                    tsl = slice(tq * 512, (tq + 1) * 512)
                    pu = bank(ui % 4)
                    tm = TM[ui % 2]
                    ui += 1
                    for kc in range(8):
                        mm(pu, Sx[:, kc, fl * 128:(fl + 1) * 128], HT[:, kc, tsl], kc == 0, kc == 7)
                    act(tm, pu, AF.Relu)
                    tm4 = tm.rearrange("p (a n) -> p a n", a=4)
                    tt("dve", UT[:, tq * 4:(tq + 1) * 4, fc, :], tm4, tm4, ALU.mult)
                if fb < 3:
                    Wu = [next_slot(), next_slot()]
                    wu_off = O_WS + ((item[0] - 2) % 4) * 8192
                    wload(Wu[0], wview("w_up", l)[:, :, (fb + 1) * 1024:(fb + 1) * 1024 + 512])
                    wload(Wu[1], wview("w_up", l)[:, :, (fb + 1) * 1024 + 512:(fb + 2) * 1024])
                def prep_down(t, fb=fb, Wd=Wd):
                    b0 = 4 + 2 * (t % 2)
                    pd = ps[:, b0:b0 + 2, :].rearrange("p a n -> p (a n)")
                    for half in range(2):
                        for fc in range(8):
                            mm(pd[:, half * 512:(half + 1) * 512], UT[:, t, fc, :], Wd[half][:, fc, :],
                               fc == 0, fc == 7)
                    if fb == 0:
                        stt("dve", ACC[:, t, :], ACC[:, t, :], ALPHA, pd, ALU.mult, ALU.add)
                    else:
                        tt("dve", ACC[:, t, :], ACC[:, t, :], pd, ALU.add)
                    return ACC[:, t, :]

                if fb < 3:
                    for t in range(16):
                        prep_down(t)
                else:
                    G, B = load_gb(wu_off, "ln_ffn_g", "ln_ffn_b", l)
                    ln_pipeline(prep_down, G, B, l == nl - 1, hook=(l < nl - 1))

        if P.stopped:
            P.stopped = False
            offs = {"HI": O_HI, "LO": O_LO, "HT": O_HT, "A": O_A, "B": O_B, "M": O_M, "WS": O_WS}
            nby = min(32768, TOTAL - offs[dump])
            src = V(offs[dump], nby, BF16)
            P.dma("pool", out.rearrange("(p a) n -> p (a n)", a=16)[:, 0:nby // 2], src, is_output=True)
        P.emit()
    return nc


def _prep_inputs(inputs):
    f = lambda a: np.ascontiguousarray(np.asarray(a))
    shared = {}
    for k in ("w_in", "w_conv_out", "w_q_b", "w_kv_b", "w_mla_o", "w_out", "ln_mix_g", "ln_mix_b",
              "w_up", "w_down", "ln_ffn_g", "ln_ffn_b"):
        shared[k] = f(inputs[k]).astype(np.float32, copy=False)
    shared["ln_in_g"] = f(inputs["ln_in_g"]).reshape(1, D)
    shared["ln_in_b"] = f(inputs["ln_in_b"]).reshape(1, D)
    shared["b_gate"] = f(np.asarray(inputs["b_gate"]).reshape(NL, 16, 128).transpose(2, 0, 1))
    shared["conv_w"] = f(np.asarray(inputs["conv_w"]).reshape(NL, 3, 8, 128).transpose(3, 0, 1, 2))
    shared["q_norm_g"] = f(np.asarray(inputs["q_norm_g"]).reshape(NL, 6, 128).transpose(2, 0, 1))
    shared["kv_norm_g"] = f(np.asarray(inputs["kv_norm_g"]).reshape(NL, 2, 128).transpose(2, 0, 1))
    invf = (1.0 / (np.float32(10000.0) ** (np.arange(0, 32, 2, dtype=np.float32) / np.float32(32)))).astype(np.float32)
    shared["invf"] = f(np.broadcast_to(invf[None, :], (128, 16)))
    shared["ident"] = np.eye(128, dtype=np.float32)
    x = np.asarray(inputs["x"])
    pos = np.asarray(inputs["positions"]).astype(np.int32)
    maps = []
    for b in range(8):
        m = dict(shared)
        m["x"] = f(x[b])
        m["pos"] = f(pos[b].reshape(16, 128).T)
        maps.append(m)
    return maps


_NC_CACHE = {}


def kernel(**inputs):
    maps = _prep_inputs(inputs)
    if "nc" not in _NC_CACHE:
        _NC_CACHE["nc"] = build(NL)
    res = run_bass_kernel_spmd(_NC_CACHE["nc"], maps, core_ids=list(range(8)))
    return np.stack([np.asarray(r["out"], dtype=np.float32) for r in res.results], axis=0)
```

```python
import contextlib
import numpy as np
import concourse.bass as bass
import concourse.mybir as mybir
from concourse.bass_utils import run_bass_kernel_spmd

F32 = mybir.dt.float32
BF16 = mybir.dt.bfloat16
I32 = mybir.dt.int32
AF = mybir.ActivationFunctionType
ALU = mybir.AluOpType
DSIZE = {F32: 4, BF16: 2, I32: 4}

ENGS = ("pe", "act", "dve", "pool", "sp")
GRAN = 512
EPOCH = 20000
NDMASEM = 8

S = 2048
D = 1024
NL = 4
ALPHA = float((2 * NL) ** 0.25)
SCALE = float(96 ** -0.5)
LN_EPS = 1e-5
RMS_EPS = 1e-6
OFF_QA = 3072
OFF_KR = 4096
OFF_G = 4128


class Prog:
    def __init__(self, nc):
        self.nc = nc
        self.recs = []
        self.streams = {e: [] for e in ENGS}
        self.lastw = {}
        self.readers = {}
        self.dma_rr = {e: 0 for e in ENGS}
        self.dma_cnt = {}
        self.out_dma = []
        self.stopped = False

    def _grans(self, ap):
        t = ap.tensor
        if "DRam" in type(t).__name__:
            return ()
        dsz = DSIZE[ap.dtype]
        rowb = int(np.prod(list(t.shape)[1:])) * DSIZE[t.dtype]
        rowe = rowb // dsz
        col = int(ap.offset) % rowe
        ext = 1
        for s, c in ap.ap[1:]:
            ext += (c - 1) * abs(s)
        lo = col * dsz
        hi = (col + ext) * dsz
        key = t.name
        return [(key, g) for g in range(lo // GRAN, (hi - 1) // GRAN + 1)]

    def _deps(self, eng, reads, writes, is_dma):
        raw = set()
        war = set()
        rg = []
        wg = []
        for ap in reads:
            rg.extend(self._grans(ap))
        for ap in writes:
            wg.extend(self._grans(ap))
        for g in rg:
            w = self.lastw.get(g)
            if w is not None:
                raw.add(w)
        for g in wg:
            w = self.lastw.get(g)
            if w is not None:
                raw.add(w)
            for r in self.readers.get(g, ()):
                war.add(r)
        gid = len(self.recs)
        for g in rg:
            self.readers.setdefault(g, []).append(gid)
        for g in wg:
            self.lastw[g] = gid
            self.readers[g] = []
        deps = set()
        for d in raw | war:
            r = self.recs[d]
            if r["dma"]:
                deps.add(d)
                continue
            if r["eng"] == eng and not is_dma and eng == "pe":
                continue
            deps.add(d)
        for d in deps:
            self.recs[d]["flag"] = True
        return deps

    def op(self, eng, fn, reads=(), writes=()):
        if self.stopped:
            return -1
        deps = self._deps(eng, reads, writes, False)
        gid = len(self.recs)
        rec = dict(eng=eng, fn=fn, deps=deps, dma=False, flag=False, gid=gid)
        self.recs.append(rec)
        self.streams[eng].append(rec)
        return gid

    def dma(self, eng, out, in_, is_output=False, **kw):
        if self.stopped:
            return -1
        deps = self._deps(eng, [in_], [out], True)
        gid = len(self.recs)
        i = self.dma_rr[eng]
        self.dma_rr[eng] = (i + 1) % NDMASEM
        sname = f"{eng}_dma{i}"
        prev = self.dma_cnt.get(sname, 0)
        self.dma_cnt[sname] = prev + 16
        rec = dict(eng=eng, fn=lambda e: e.dma_start(out=out, in_=in_, **kw), deps=deps, dma=True,
                   flag=True, gid=gid, dsem=sname, dval=prev + 16, dprev=prev)
        self.recs.append(rec)
        self.streams[eng].append(rec)
        if is_output:
            self.out_dma.append(gid)
        return gid

    def emit(self):
        nc = self.nc
        for e in ENGS:
            k = 0
            for r in self.streams[e]:
                if r["flag"] and not r["dma"]:
                    k += 1
                    r["ord"] = k
        semnames = set()
        for e in ENGS:
            n = sum(1 for r in self.streams[e] if r["flag"] and not r["dma"])
            for ep in range(n // EPOCH + 1):
                semnames.add(f"{e}_c{ep}")
        for s in self.dma_cnt:
            semnames.add(s)
        semnames = sorted(semnames)
        with contextlib.ExitStack() as st:
            sems = {s: st.enter_context(nc.semaphore(s)) for s in semnames}
            block = st.enter_context(nc.Block())

            def token_wait(d):
                r = self.recs[d]
                if r["dma"]:
                    return (r["dsem"], r["dval"], None)
                o = r["ord"] - 1
                return (f"{r['eng']}_c{o // EPOCH}", o % EPOCH + 1, (r["eng"], o))

            def run_stream(e, engine):
                waited_eng = {}
                waited_dma = {}
                for r in self.streams[e]:
                    best = {}
                    for d in r["deps"]:
                        sname, val, eo = token_wait(d)
                        if eo is None:
                            if waited_dma.get(sname, 0) >= val:
                                continue
                            waited_dma[sname] = val
                        else:
                            pe_, o = eo
                            if waited_eng.get(pe_, -1) >= o:
                                continue
                            waited_eng[pe_] = o
                        best[sname] = max(best.get(sname, 0), val)
                    if r["dma"] and r["dprev"] > 0:
                        if waited_dma.get(r["dsem"], 0) < r["dprev"]:
                            waited_dma[r["dsem"]] = r["dprev"]
                            best[r["dsem"]] = max(best.get(r["dsem"], 0), r["dprev"])
                    for sname, val in best.items():
                        engine.wait_ge(sems[sname], val)
                    inst = r["fn"](engine)
                    if r["dma"]:
                        inst.then_inc(sems[r["dsem"]], 16)
                    elif r["flag"]:
                        o = r["ord"] - 1
                        inst.then_inc(sems[f"{e}_c{o // EPOCH}"], 1)
                if e == "sp":
                    for d in self.out_dma:
                        rr = self.recs[d]
                        engine.wait_ge(sems[rr["dsem"]], rr["dval"])

            @block.tensor
            def _(eng):
                run_stream("pe", eng)

            @block.scalar
            def _(eng):
                run_stream("act", eng)

            @block.vector
            def _(eng):
                run_stream("dve", eng)

            @block.gpsimd
            def _(eng):
                run_stream("pool", eng)

            @block.sync
            def _(eng):
                run_stream("sp", eng)


class Cut(Exception):
    pass


def build(nl=NL, cut=None, dump=None):
    nc = bass.Bass("TRN2", target_bir_lowering=False)
    Dm = {}

    def din(name, shape, dt=F32):
        Dm[name] = nc.dram_tensor(name, shape, dt, kind="ExternalInput").ap()

    din("x", [S, D])
    din("pos", [128, 16], I32)
    din("invf", [128, 16])
    din("ident", [128, 128])
    din("ln_in_g", [1, D])
    din("ln_in_b", [1, D])
    din("w_in", [NL, D, 6176])
    din("b_gate", [128, NL, 16])
    din("conv_w", [128, NL, 3, 8])
    din("w_conv_out", [NL, D, D])
    din("q_norm_g", [128, NL, 6])
    din("w_q_b", [NL, 768, 1536])
    din("kv_norm_g", [128, NL, 2])
    din("w_kv_b", [NL, 256, 2048])
    din("w_mla_o", [NL, D, D])
    din("w_out", [NL, D, D])
    din("ln_mix_g", [NL, D])
    din("ln_mix_b", [NL, D])
    din("w_up", [NL, D, 4 * D])
    din("w_down", [NL, 4 * D, D])
    din("ln_ffn_g", [NL, D])
    din("ln_ffn_b", [NL, D])
    out = nc.dram_tensor("out", [S, D], F32, kind="ExternalOutput").ap()

    TOTAL = 211968
    with contextlib.ExitStack() as st:
        ar = st.enter_context(nc.sbuf_tensor("arena", [128, TOTAL // 2], BF16))
        ps = st.enter_context(nc.psum_tensor("ps", [128, 8, 512], F32))
        P = Prog(nc)

        def ck(name):
            if cut == name:
                P.stopped = True

        def V(off, nbytes, dt=BF16, pat=None, **kw):
            assert off % 4 == 0
            a = ar[:, off // 2:(off + nbytes) // 2]
            if dt != BF16:
                a = a.bitcast(dt)
            if pat is not None:
                a = a.rearrange(pat, **kw)
            return a

        O_HI, O_LO, O_HT, O_A, O_B, O_WS, O_M = 0, 32768, 65536, 98304, 131072, 163840, 196608
        HI = V(O_HI, 32768, BF16, "p (a n) -> p a n", a=16)
        LO = V(O_LO, 32768, BF16, "p (a n) -> p a n", a=16)
        HT = V(O_HT, 32768, BF16, "p (a n) -> p a n", a=8)
        WS = [V(O_WS + i * 8192, 8192, BF16, "p (a n) -> p a n", a=8) for i in range(4)]
        IDB = V(O_M + 0, 256)
        ONESB = V(O_M + 512, 256)
        ONES32 = V(O_M + 1024, 512, F32)
        COS = V(O_M + 1536, 1024, F32, "p (a n) -> p a n", a=16)
        SIN = V(O_M + 2560, 1024, F32, "p (a n) -> p a n", a=16)
        COSQ = V(O_M + 3584, 1024, F32, "p (a n) -> p a n", a=16)
        SINQ = V(O_M + 4608, 1024, F32, "p (a n) -> p a n", a=16)
        BGATE = V(O_M + 5632, 256, F32, "p (a n) -> p a n", a=NL)
        CONVW = V(O_M + 5888, 384, F32, "p (l k c) -> p l k c", l=NL, k=3)
        QG = V(O_M + 6272, 96, F32, "p (a n) -> p a n", a=NL)
        KVG = V(O_M + 6368, 32, F32, "p (a n) -> p a n", a=NL)
        O_ST = O_M + 6656
        O_SC = O_M + 7680
        assert O_SC + 7680 == TOTAL

        def bank(b, n=512):
            return ps[:, b, 0:n]

        def bankbf(b):
            return ps[:, b, :].bitcast(BF16)

        def mm(out, lhsT, rhs, start, stop):
            P.op("pe", lambda e: e.matmul(out, lhsT=lhsT, rhs=rhs, start=start, stop=stop),
                 reads=[lhsT, rhs], writes=[out])

        def tr(out, in_):
            np_ = in_.shape[0]
            idn = IDB[0:np_, 0:np_]
            P.op("pe", lambda e: e.transpose(out=out, in_=in_, identity=idn), reads=[in_, idn], writes=[out])

        def tt(eng, out, in0, in1, op):
            P.op(eng, lambda e: e.tensor_tensor(out=out, in0=in0, in1=in1, op=op), reads=[in0, in1], writes=[out])

        def ts(eng, out, in0, s1, s2, op0, op1=None):
            rd = [in0] + [s for s in (s1, s2) if not isinstance(s, (int, float, type(None)))]
            if op1 is None:
                P.op(eng, lambda e: e.tensor_single_scalar(out=out, in_=in0, scalar=s1, op=op0), reads=rd, writes=[out])
            else:
                P.op(eng, lambda e: e.tensor_scalar(out=out, in0=in0, scalar1=s1, scalar2=s2, op0=op0, op1=op1),
                     reads=rd, writes=[out])

        def stt(eng, out, in0, sc, in1, op0, op1):
            rd = [in0, in1] + ([sc] if not isinstance(sc, (int, float)) else [])
            P.op(eng, lambda e: e.scalar_tensor_tensor(out=out, in0=in0, scalar=sc, in1=in1, op0=op0, op1=op1),
                 reads=rd, writes=[out])

        def cp(eng, out, in_):
            if eng == "act":
                P.op("act", lambda e: e.copy(out=out, in_=in_), reads=[in_], writes=[out])
            else:
                P.op(eng, lambda e: e.tensor_copy(out=out, in_=in_), reads=[in_], writes=[out])

        def act(out, in_, func, bias=None, scale=None):
            kw = {}
            rd = [in_]
            if bias is not None:
                kw["bias"] = bias
                if not isinstance(bias, (int, float)):
                    rd.append(bias)
            if scale is not None:
                kw["scale"] = scale
            P.op("act", lambda e: e.activation(out=out, in_=in_, func=func, **kw), reads=rd, writes=[out])

        def memset(eng, ap, val):
            P.op(eng, lambda e: e.memset(ap, val), reads=[], writes=[ap])

        evac_rr = [0]

        evac_mode = [None]

        def evac(out, in_):
            if evac_mode[0] is not None:
                cp(evac_mode[0], out, in_)
                return
            evac_rr[0] ^= 1
            cp("act" if evac_rr[0] else "dve", out, in_)

        def wview(name, l):
            return Dm[name][l].rearrange("(kc p) n -> p kc n", p=128)

        def wload(slot, src):
            P.dma("pool", slot, src)

        tmp32 = V(O_SC, 512, F32)
        P.dma("sp", tmp32, Dm["ident"])
        cp("dve", IDB, tmp32)
        memset("dve", ONESB, 1.0)
        memset("dve", ONES32, 1.0)
        P.dma("sp", BGATE, Dm["b_gate"])
        P.dma("sp", CONVW, Dm["conv_w"])
        P.dma("sp", QG, Dm["q_norm_g"])
        P.dma("sp", KVG, Dm["kv_norm_g"])
        posi = V(O_SC + 512, 64, I32)
        posf = V(O_SC + 1024, 64, F32)
        invf = V(O_SC + 1536, 64, F32)
        ang = V(O_SC + 2048, 1024, F32, "p (a n) -> p a n", a=16)
        arg = V(O_SC + 3072, 1024, F32, "p (a n) -> p a n", a=16)
        P.dma("sp", posi, Dm["pos"])
        P.dma("sp", invf, Dm["invf"])
        cp("dve", posf, posi)
        for t in range(16):
            ts("dve", ang[:, t, :], invf, posf[:, t:t + 1], None, ALU.mult)
        PI = float(np.pi)
        kf = V(O_SC + 4096, 1024, F32, "p (a n) -> p a n", a=16)
        ki = V(O_SC + 5120, 1024, I32, "p (a n) -> p a n", a=16)
        msk = V(O_SC + 6144, 1024, F32, "p (a n) -> p a n", a=16)
        for tab, shift in ((SIN, 0.0), (COS, 0.5 * PI)):
            ts("dve", arg, ang, shift, None, ALU.add)
            ts("dve", kf, arg, 1.0 / (2 * PI), None, ALU.mult)
            cp("dve", ki, kf)
            cp("dve", kf, ki)
            stt("dve", arg, kf, -2 * PI, arg, ALU.mult, ALU.add)
            ts("dve", msk, arg, PI, None, ALU.is_gt)
            stt("dve", arg, msk, -2 * PI, arg, ALU.mult, ALU.add)
            ts("dve", msk, arg, -PI, None, ALU.is_lt)
            stt("dve", arg, msk, 2 * PI, arg, ALU.mult, ALU.add)
            ts("dve", arg, arg, -PI, PI, ALU.max, ALU.min)
            act(tab, arg, AF.Sin)
        ts("dve", COSQ, COS, SCALE, None, ALU.mult)
        ts("dve", SINQ, SIN, SCALE, None, ALU.mult)

        ck("setup")

        def bc_heads(tab, t0):
            v = tab[:, t0:t0 + 2, :]
            a = [list(x) for x in v.ap]
            return bass.AP(v.tensor, v.offset, [a[0], a[1], [0, 2], a[2]])

        def ln_stage1(R, t, eps=LN_EPS):
            so = O_ST + (t % 2) * 512
            bn = V(so, 48, F32, "p (a n) -> p a n", a=2)
            mv = V(so + 64, 8, F32)
            rstd = V(so + 96, 4, F32)
            P.op("dve", lambda e: e.bn_stats(out=bn[:, 0, :], in_=R[:, 0:512]), reads=[R[:, 0:512]], writes=[bn[:, 0, :]])
            P.op("dve", lambda e: e.bn_stats(out=bn[:, 1, :], in_=R[:, 512:1024]), reads=[R[:, 512:1024]], writes=[bn[:, 1, :]])
            P.op("dve", lambda e: e.bn_aggr(out=mv, in_=bn), reads=[bn], writes=[mv])
            ts("dve", rstd, mv[:, 1:2], eps, None, ALU.add)
            act(rstd, rstd, AF.Sqrt)

        def ln_stage2(Rin, R, G, B, t, final):
            so = O_ST + (t % 2) * 512
            mv = V(so + 64, 8, F32)
            rstd = V(so + 96, 4, F32)
            P.op("dve", lambda e: e.reciprocal(out=rstd, in_=rstd), reads=[rstd], writes=[rstd])
            stt("dve", R, Rin, mv[:, 0:1], G, ALU.subtract, ALU.mult)
            stt("dve", R, R, rstd, B, ALU.mult, ALU.add)
            if final:
                P.dma("sp", out[t * 128:(t + 1) * 128, :], R, is_output=True)
            else:
                cp("act", HI[:, t, :], R)
                tt("pool", LO[:, t, :], R, HI[:, t, :], ALU.subtract)

        def ln_pipeline(prep, G, B, final, eps=LN_EPS, hook=True):
            Rs = {}
            for t in range(17):
                if t < 16:
                    r = prep(t)
                    Rs[t] = r if isinstance(r, tuple) else (r, r)
                    ln_stage1(Rs[t][0], t, eps)
                if t >= 1:
                    ln_stage2(Rs[t - 1][0], Rs[t - 1][1], G, B, t - 1, final)
                    if hook:
                        ln_loop_hook(t - 1)

        def load_gb(off, gname, bname, l):
            G = V(off, 4096, F32)
            B = V(off + 4096, 4096, F32)
            gs = Dm[gname][l:l + 1, :] if l is not None else Dm[gname]
            bs = Dm[bname][l:l + 1, :] if l is not None else Dm[bname]
            P.dma("sp", G.unsqueeze(1), gs.partition_broadcast(128))
            P.dma("sp", B.unsqueeze(1), bs.partition_broadcast(128))
            return G, B

        hb = [0]

        hT_banks = [(0, 1, 2, 3)]

        def build_hT_block(tq):
            for kc in range(8):
                bl = hT_banks[0]
                b = bl[hb[0] % len(bl)]
                hb[0] += 1
                pt = bankbf(b)[:, 0:512].rearrange("p (a n) -> p a n", a=4)
                for j in range(4):
                    tr(pt[:, j, :], HI[:, tq * 4 + j, kc * 128:(kc + 1) * 128])
                evac(HT[:, kc, tq * 512:(tq + 1) * 512], bankbf(b)[:, 0:512])

        def build_hT():
            for tq in range(4):
                build_hT_block(tq)

        def ln_loop_hook(t):
            evac_mode[0] = "act"
            if t >= 6 and (t - 3) % 4 == 3:
                build_hT_block((t - 3) // 4)
            if t == 15:
                build_hT_block(3)
            evac_mode[0] = None

        G0, B0 = load_gb(O_B, "ln_in_g", "ln_in_b", None)
        def prep_entry(t):
            X = V(O_B + 8192 + (t % 4) * 4096, 4096, F32)
            P.dma("sp", X, Dm["x"][t * 128:(t + 1) * 128, :])
            return X

        ln_pipeline(prep_entry, G0, B0, False)
        ck("entryln")
        item = [0]

        def next_slot():
            s = WS[item[0] % 4]
            item[0] += 1
            return s

        for l in range(nl):
            w_in = wview("w_in", l)
            ck("hT")
            base = item[0] % 4
            item[0] += 4 + (0 if l == 0 else 2)
            sh = 0 if l == 0 else 2
            base_slot = (base + sh) % 4
            S1 = WS[(base + sh) % 4]; S2 = WS[(base + sh + 1) % 4]
            A0 = WS[(base + sh + 2) % 4]; Bg0 = WS[(base + sh + 3) % 4]
            wload(S1, w_in[:, :, OFF_QA:OFF_QA + 512])
            wload(S2, w_in[:, :, OFF_QA + 512:OFF_QA + 1024])
            KRW = V(O_SC, 512, BF16, "p (a n) -> p a n", a=8)
            wload(KRW, w_in[:, :, OFF_KR:OFF_KR + 32])
            wload(A0, wview("w_mla_o", l)[:, :, 0:512])
            wload(Bg0, w_in[:, :, OFF_G + 1024:OFF_G + 1536])
            ck("wl")
            QKVN = V(O_A, 32768, BF16, "p (a n) -> p a n", a=8)
            RAW = V(O_B, 16384, F32, "p (a n) -> p a n", a=8)
            SQ = V(O_B + 16384, 8192, BF16, "p (a n) -> p a n", a=8)
            RSTD = V(O_B + 24576, 4096, F32, "p (a n) -> p a n", a=2)
            KRTM = V(O_B + 28672, 3072, BF16, "p (a n) -> p a n", a=16)
            memset("dve", KRTM, 0.0)
            bi = 0
            for tq in range(4):
                tsl = slice(tq * 512, (tq + 1) * 512)
                for j in range(8):
                    b = bi % 4
                    bi += 1
                    Sx = S1 if j < 4 else S2
                    jl = j % 4
                    for kc in range(8):
                        mm(bank(b), Sx[:, kc, jl * 128:(jl + 1) * 128], HT[:, kc, tsl], kc == 0, kc == 7)
                    ck("p1m")
                    cp("dve", RAW[:, j, :], bank(b))
                    ck("p1c")
                    tt("dve", SQ[:, j, :], RAW[:, j, :], RAW[:, j, :], ALU.mult)
                ck("p1j")
                for gi, js, nf in ((0, list(range(6)), 768.0), (1, [6, 7], 256.0)):
                    pb = bank(4 + gi)
                    for idx, j in enumerate(js):
                        mm(pb, ONESB, SQ[:, j, :], idx == 0, idx == len(js) - 1)
                    ts("dve", RSTD[:, gi, :], pb, 1.0 / nf, RMS_EPS, ALU.mult, ALU.add)
                    act(RSTD[:, gi, :], RSTD[:, gi, :], AF.Ln)
                    act(RSTD[:, gi, :], RSTD[:, gi, :], AF.Exp, scale=-0.5)
                ck("p1s")
                for j in range(8):
                    gi = 0 if j < 6 else 1
                    gcol = QG[:, l, j:j + 1] if j < 6 else KVG[:, l, j - 6:j - 5]
                    stt("dve", QKVN[:, j, tsl], RAW[:, j, :], gcol, RSTD[:, gi, :], ALU.mult, ALU.mult)
                ck("p1n")
                pk = ps[:, 6, 0:128].rearrange("p (a n) -> p a n", a=4)
                for j in range(4):
                    t = tq * 4 + j
                    for kc in range(8):
                        mm(pk[:, j, :], HT[:, kc, t * 128:(t + 1) * 128], KRW[:, kc, :], kc == 0, kc == 7)
                ck("p1k")
                x1 = pk[:, :, 0:16]
                x2 = pk[:, :, 16:32]
                c_ = COS[:, tq * 4:tq * 4 + 4, :]
                s_ = SIN[:, tq * 4:tq * 4 + 4, :]
                T = [V(O_SC + 1024 + i * 512, 256, F32, "p (a n) -> p a n", a=4) for i in range(4)]
                tsl4 = slice(tq * 4, tq * 4 + 4)
                tt("dve", T[0], x1, c_, ALU.mult)
                tt("dve", T[1], x2, s_, ALU.mult)
                tt("dve", KRTM[:, tsl4, 64:80], T[0], T[1], ALU.subtract)
                tt("dve", T[2], x1, s_, ALU.mult)
                tt("dve", T[3], x2, c_, ALU.mult)
                tt("dve", KRTM[:, tsl4, 80:96], T[2], T[3], ALU.add)

            ck("p1")
            AT = V(O_B, 32768, BF16, "p (a n) -> p a n", a=8)
            o_t2 = O_WS + base_slot * 8192
            QT = [V(O_HT + i * 8192, 8192, BF16, "p (a n) -> p a n", a=2) for i in range(2)]
            KT = [V(O_HT + 16384 + i * 4096, 4096, BF16) for i in range(2)]
            PT = [V(O_HT + 24576 + i * 1024, 1024, BF16) for i in range(3)]
            OS = V(O_HT + 27648, 2048, F32)
            QTM = [V(O_HT + 29696 + i * 768, 768, BF16, "p (a h d) -> p a h d", a=2, h=2) for i in range(2)]
            RT = [V(O_HT + 31232 + i * 256, 256, F32, "p (a h d) -> p a h d", a=2, h=2) for i in range(4)]
            RB = V(o_t2 + 12288, 1024, BF16)
            LNR = V(o_t2 + 13312, 2048, F32)
            VP = [V(o_t2 + i * 6144, 6144, BF16, "p (a n) -> p a n", a=16) for i in range(2)]
            WQb = [V(O_SC + i * 3584, 2304, BF16, "p (a n) -> p a n", a=6) for i in range(2)]
            WKVb = [V(O_SC + i * 3584 + 2560, 1024, BF16, "p (a n) -> p a n", a=2) for i in range(2)]
            for tq in range(4):
                ptb = bankbf(7)[:, 0:512]
                pt4 = ptb.rearrange("p (a n) -> p a n", a=4)
                for j in range(4):
                    tr(pt4[0:96, j, :], KRTM[:, tq * 4 + j, :])
                cp("dve", KT[0][64:96, tq * 512:(tq + 1) * 512], ptb[64:96, :])
                cp("dve", KT[1][64:96, tq * 512:(tq + 1) * 512], ptb[64:96, :])
            for i in range(2):
                memset("pool", VP[i][:, :, 64:128], 0.0)
                memset("pool", VP[i][:, :, 64:65], 1.0)

            wq_all = Dm["w_q_b"][l].rearrange("(kc p) n -> p kc n", p=128)
            wkv_all = Dm["w_kv_b"][l].rearrange("(kc p) n -> p kc n", p=128)

            prb = [0]

            def prod_bank():
                prb[0] ^= 1
                return 6 + prb[0]

            def load_pair_w(hp):
                wload(WQb[hp % 2], wq_all[:, :, hp * 192:(hp + 1) * 192])
                wload(WKVb[hp % 2], wkv_all[:, :, hp * 256:(hp + 1) * 256])

            def prod_qv(hp):
                par = hp % 2
                WQ, WKV = WQb[par], WKVb[par]
                yield
                for tq in range(4):
                    pT = bankbf(5).rearrange("p (h n) -> p h n", h=2)
                    for half in range(2):
                        t0 = tq * 4 + half * 2
                        bq = prod_bank()
                        for a in range(2):
                            for kc in range(6):
                                mm(ps[:, bq, a * 192:(a + 1) * 192], QKVN[:, kc, (t0 + a) * 128:(t0 + a + 1) * 128],
                                   WQ[:, kc, :], kc == 0, kc == 5)
                        yield
                        pq = ps[:, bq, 0:384].rearrange("p (a h d) -> p a h d", a=2, h=2)
                        qm = QTM[half]
                        ts("dve", qm[:, :, :, 0:64], pq[:, :, :, 0:64], SCALE, None, ALU.mult)
                        x1 = pq[:, :, :, 64:80]
                        x2 = pq[:, :, :, 80:96]
                        cq = bc_heads(COSQ, t0)
                        sq_ = bc_heads(SINQ, t0)
                        tt("dve", RT[0], x1, cq, ALU.mult)
                        tt("dve", RT[1], x2, sq_, ALU.mult)
                        tt("dve", qm[:, :, :, 64:80], RT[0], RT[1], ALU.subtract)
                        tt("dve", RT[2], x1, sq_, ALU.mult)
                        tt("dve", RT[3], x2, cq, ALU.mult)
                        tt("dve", qm[:, :, :, 80:96], RT[2], RT[3], ALU.add)
                        for _ in range(7):
                            yield
                        for a in range(2):
                            for h in range(2):
                                c0 = (half * 2 + a) * 128
                                tr(pT[0:96, h, c0:c0 + 128], qm[:, a, h, :])
                        yield
                    yield
                    cp("dve", QT[par][0:96, :, tq * 512:(tq + 1) * 512], pT[0:96, :, :])
                    yield
                for t4 in range(4):
                    bv = prod_bank()
                    pv = ps[:, bv, :].rearrange("p (a h d) -> p a h d", a=4, h=2)
                    for a in range(4):
                        t = t4 * 4 + a
                        for kc in range(2):
                            rv = WKV[:, kc, :].rearrange("p (h t d) -> p h t d", h=2, t=2)[:, :, 1, :]
                            mm(pv[:, a, :, :], QKVN[:, 6 + kc, t * 128:(t + 1) * 128], rv, kc == 0, kc == 1)
                    yield
                    yield
                    cp("dve", VP[par][:, t4 * 4:t4 * 4 + 4, 0:64], pv[:, :, 0, :])
                    cp("dve", VP[par][:, t4 * 4:t4 * 4 + 4, 128:192], pv[:, :, 1, :])
                    yield

            def prod_k(n):
                hp, hl = n // 2, n % 2
                WKV = WKVb[hp % 2]
                for tq in range(4):
                    bk = prod_bank()
                    pk_ = ps[0:64, bk, :]
                    for kc in range(2):
                        mm(pk_, WKV[:, kc, hl * 128:hl * 128 + 64], QKVN[:, 6 + kc, tq * 512:(tq + 1) * 512],
                           kc == 0, kc == 1)
                    yield
                    yield
                    cp("dve", KT[hl][0:64, tq * 512:(tq + 1) * 512], pk_)
                    yield

            def drain(g):
                if g is not None:
                    for _ in g:
                        pass

            load_pair_w(0)
            load_pair_w(1)
            drain(prod_qv(0))
            drain(prod_k(0))

            steps = [(n, tq, kt) for n in range(16) for tq in range(4) for kt in range(16)]
            NS = len(steps)

            def rec_S(si):
                n, tq, kt = steps[si]
                hp, hl = n // 2, n % 2
                mm(bank(si % 3), KT[hl][0:96, kt * 128:(kt + 1) * 128], QT[hp % 2][0:96, hl, tq * 512:(tq + 1) * 512],
                   True, True)

            gen_qv = None
            gen_k = None
            deferred = []
            rec_S(0)
            rec_S(1)
            oi = 0
            for si in range(NS):
                n, tq, kt = steps[si]
                hp, hl = n // 2, n % 2
                if tq == 0 and kt == 0:
                    drain(gen_k)
                    gen_k = prod_k(n + 1) if n + 1 < 16 else None
                    if hl == 0:
                        drain(gen_qv)
                        gen_qv = prod_qv(hp + 1) if hp + 1 < 8 else None
                    elif hp + 2 < 8:
                        load_pair_w(hp + 2)
                pt_ = PT[si % 3]
                act(pt_, bank(si % 3), AF.Exp)
                if si + 2 < NS:
                    n2 = steps[si + 2][0]
                    if n2 != n:
                        drain(gen_k); gen_k = None
                        if n2 % 2 == 0:
                            drain(gen_qv); gen_qv = None
                    rec_S(si + 2)
                po = ps[:, 3 + (oi % 2), :]
                if hl == 0:
                    mm(po[0:65, :], VP[hp % 2][:, kt, 0:65], pt_, kt == 0, kt == 15)
                else:
                    mm(po[:, :], VP[hp % 2][:, kt, 64:192], pt_, kt == 0, kt == 15)
                for dd in deferred:
                    dd[0] -= 1
                for dd in [d_ for d_ in deferred if d_[0] <= 0]:
                    deferred.remove(dd)
                    dd[1]()
                if kt == 15:
                    tsl = slice(tq * 512, (tq + 1) * 512)
                    if hl == 0:
                        def part1(po=po):
                            act(LNR[64:65, :], po[64:65, :], AF.Ln)
                            act(RB[64:65, :], LNR[64:65, :], AF.Exp, scale=-1.0)
                            cp("dve", OS[0:64, :], po[0:64, :])

                        def part2(hp=hp, tsl=tsl, pbc=po):
                            mm(pbc[0:64, :], ONESB[64:65, 0:64], RB[64:65, :], True, True)
                            tt("dve", AT[0:64, hp, tsl], OS[0:64, :], pbc[0:64, :], ALU.mult)
                    else:
                        def part1(po=po):
                            act(LNR[0:1, :], po[0:1, :], AF.Ln)
                            act(RB[0:1, :], LNR[0:1, :], AF.Exp, scale=-1.0)
                            cp("dve", OS[64:128, :], po[64:128, :])

                        def part2(hp=hp, tsl=tsl, pbc=po):
                            mm(pbc[:, :], ONESB[0:1, :], RB[0:1, :], True, True)
                            tt("dve", AT[64:128, hp, tsl], OS[64:128, :], pbc[64:128, :], ALU.mult)
                    deferred.append([2, part1])
                    deferred.append([8, part2])
                    oi += 1
                for g in (gen_qv, gen_k):
                    if g is not None:
                        next(g, None)
            while deferred:
                deferred.pop(0)[1]()

            ck("p2")
            MT = V(O_A, 32768, BF16, "p (a n) -> p a n", a=8)
            GT = [V(O_SC + i * 2048, 2048, F32) for i in range(2)]
            build_hT_block(0)
            A1 = next_slot(); wload(A1, wview("w_mla_o", l)[:, :, 512:1024])
            Bg1 = next_slot(); wload(Bg1, w_in[:, :, OFF_G + 1536:OFF_G + 2048])
            gi_ = 0
            Wc = [None, None]; Wh = [None, None]; Wb = [None, None]
            for cp_ in range(2):
                A_, B_ = (A0, Bg0) if cp_ == 0 else (A1, Bg1)
                if cp_ == 1:
                    Wc[0] = next_slot(); wload(Wc[0], w_in[:, :, 1024:1536])
                    Wh[0] = next_slot(); wload(Wh[0], w_in[:, :, 2048:2560])
                order3 = [(c, tq) for tq in range(4) for c in range(4)] if cp_ == 0 else \
                         [(c, tq) for c in range(4) for tq in range(4)]
                for c, tq in order3:
                    if cp_ == 0 and c == 0 and tq + 1 < 4:
                        build_hT_block(tq + 1)
                    cabs = cp_ * 4 + c
                    tsl = slice(tq * 512, (tq + 1) * 512)
                    py = bank(4 + gi_ % 2)
                    pg = bank(6 + gi_ % 2)
                    g_ = GT[gi_ % 2]
                    gi_ += 1
                    for kc in range(8):
                        mm(py, A_[:, kc, c * 128:(c + 1) * 128], AT[:, kc, tsl], kc == 0, kc == 7)
                    for kc in range(8):
                        mm(pg, B_[:, kc, c * 128:(c + 1) * 128], HT[:, kc, tsl], kc == 0, kc == 7)
                    act(g_, pg, AF.Sigmoid, bias=BGATE[:, l, 8 + cabs:9 + cabs])
                    stt("dve", MT[:, cabs, tsl], g_, 1.0 / ALPHA, py, ALU.mult, ALU.mult)

            ck("p3")
            ZT = V(O_B, 32768, BF16, "p (a n) -> p a n", a=8)
            U = V(O_SC, 4100, BF16)
            TC = V(O_SC + 4608, 2048, F32)
            memset("pool", U[:, 0:2], 0.0)
            memset("pool", U[:, 2048:2050], 0.0)
            Wb[0] = next_slot(); wload(Wb[0], w_in[:, :, 0:512])
            Wc[1] = next_slot(); wload(Wc[1], w_in[:, :, 1536:2048])
            gi_ = 0
            for sc in range(2):
                if sc == 1:
                    Wh[1] = next_slot(); wload(Wh[1], w_in[:, :, 2560:3072])
                    Wb[1] = next_slot(); wload(Wb[1], w_in[:, :, 512:1024])
                for c in range(4):
                    cabs = sc * 4 + c
                    csl = slice(c * 128, (c + 1) * 128)
                    for tq in range(4):
                        tsl = slice(tq * 512, (tq + 1) * 512)
                        pc = bank(gi_ % 2)
                        ph = bank(2 + gi_ % 2)
                        gi_ += 1
                        for kc in range(8):
                            mm(pc, Wc[sc][:, kc, csl], HT[:, kc, tsl], kc == 0, kc == 7)
                        for kc in range(8):
                            mm(ph, Wh[sc][:, kc, csl], HT[:, kc, tsl], kc == 0, kc == 7)
                        usl = U[:, 1 + tq * 512:1 + (tq + 1) * 512]
                        cp("act", usl, pc)
                        tt("dve", usl, usl, ph, ALU.mult)
                    for tq in range(4):
                        tsl = slice(tq * 512, (tq + 1) * 512)
                        pb = bank(4 + tq % 2)
                        for kc in range(8):
                            mm(pb, Wb[sc][:, kc, csl], HT[:, kc, tsl], kc == 0, kc == 7)
                        w0 = CONVW[:, l, 0, cabs:cabs + 1]
                        w1 = CONVW[:, l, 1, cabs:cabs + 1]
                        w2 = CONVW[:, l, 2, cabs:cabs + 1]
                        ts("dve", TC, U[:, tq * 512:tq * 512 + 512], w0, None, ALU.mult)
                        stt("dve", TC, U[:, tq * 512 + 1:tq * 512 + 513], w1, TC, ALU.mult, ALU.add)
                        stt("dve", TC, U[:, tq * 512 + 2:tq * 512 + 514], w2, TC, ALU.mult, ALU.add)
                        tt("dve", ZT[:, cabs, tsl], TC, pb, ALU.mult)

            ck("p4a")
            GT = [V(O_SC + i * 2048, 2048, F32) for i in range(2)]
            T2 = V(O_SC + 4096, 2048, F32)
            Wco = [None, None]; Wg = [None, None]
            Wco[0] = next_slot(); wload(Wco[0], wview("w_conv_out", l)[:, :, 0:512])
            Wg[0] = next_slot(); wload(Wg[0], w_in[:, :, OFF_G:OFF_G + 512])
            gi_ = 0
            for cp_ in range(2):
                if cp_ == 1:
                    Wco[1] = next_slot(); wload(Wco[1], wview("w_conv_out", l)[:, :, 512:1024])
                    Wg[1] = next_slot(); wload(Wg[1], w_in[:, :, OFF_G + 512:OFF_G + 1024])
                for c in range(4):
                    cabs = cp_ * 4 + c
                    csl = slice(c * 128, (c + 1) * 128)
                    for tq in range(4):
                        tsl = slice(tq * 512, (tq + 1) * 512)
                        py = bank(gi_ % 2)
                        pg = bank(2 + gi_ % 2)
                        g_ = GT[gi_ % 2]
                        gi_ += 1
                        for kc in range(8):
                            mm(pg, Wg[cp_][:, kc, csl], HT[:, kc, tsl], kc == 0, kc == 7)
                        for kc in range(8):
                            mm(py, Wco[cp_][:, kc, csl], ZT[:, kc, tsl], kc == 0, kc == 7)
                        act(g_, pg, AF.Sigmoid, bias=BGATE[:, l, cabs:cabs + 1])
                        stt("dve", T2, g_, 1.0 / ALPHA, py, ALU.mult, ALU.mult)
                        tt("dve", MT[:, cabs, tsl], T2, MT[:, cabs, tsl], ALU.add)

            ck("p4b")
            Wo = [next_slot(), next_slot()]
            wload(Wo[0], wview("w_out", l)[:, :, 0:512])
            wload(Wo[1], wview("w_out", l)[:, :, 512:1024])
            Wu = [next_slot(), next_slot()]
            wload(Wu[0], wview("w_up", l)[:, :, 0:512])
            wload(Wu[1], wview("w_up", l)[:, :, 512:1024])
            G, B = load_gb(O_B, "ln_mix_g", "ln_mix_b", l)
            def prep_mix(t):
                b0 = 2 + 2 * (t % 3)
                pm = ps[:, b0:b0 + 2, :].rearrange("p a n -> p (a n)")
                R = V(O_B + 8192 + (t % 4) * 4096, 4096, F32)
                for half in range(2):
                    hs = slice(half * 512, (half + 1) * 512)
                    for kc in range(8):
                        mm(pm[:, hs], MT[:, kc, t * 128:(t + 1) * 128], Wo[half][:, kc, :], kc == 0, False)
                    mm(pm[:, hs], IDB, HI[:, t, hs], False, False)
                    mm(pm[:, hs], IDB, LO[:, t, hs], False, True)
                return (pm, R)

            hT_banks[0] = (0, 1)
            ln_pipeline(prep_mix, G, B, False, eps=LN_EPS / (ALPHA * ALPHA))
            hT_banks[0] = (0, 1, 2, 3)

            ck("p5")
            ACC = V(O_A, 65536, F32, "p (a n) -> p a n", a=16)
            for t in range(16):
                tt("dve", ACC[:, t, :], HI[:, t, :], LO[:, t, :], ALU.add)
            UT = V(O_HI, 32768, BF16, "p (t f n) -> p t f n", t=16, f=8)
            TM = [V(O_SC + i * 2048, 2048, F32) for i in range(2)]
            ui = 0
            wu_off = 0
            for fb in range(4):
                Wd = [next_slot(), next_slot()]
                wd_off = O_WS + ((item[0] - 2) % 4) * 8192
                wd_v = Dm["w_down"][l, fb * 1024:(fb + 1) * 1024, :].rearrange("(fc p) n -> p fc n", p=128)
                wload(Wd[0], wd_v[:, :, 0:512])
                wload(Wd[1], wd_v[:, :, 512:1024])
                order = [(fc, tq) for tq in range(4) for fc in range(8)] if fb == 0 else \
                        [(fc, tq) for fc in range(8) for tq in range(4)]
                for fc, tq in order:
                    Sx = Wu[fc // 4]
                    fl = fc % 4
                    tsl = slice(tq * 512, (tq + 1) * 512)
                    pu = bank(ui % 4)
                    tm = TM[ui % 2]
                    ui += 1
                    for kc in range(8):
                        mm(pu, Sx[:, kc, fl * 128:(fl + 1) * 128], HT[:, kc, tsl], kc == 0, kc == 7)
                    act(tm, pu, AF.Relu)
                    tm4 = tm.rearrange("p (a n) -> p a n", a=4)
                    tt("dve", UT[:, tq * 4:(tq + 1) * 4, fc, :], tm4, tm4, ALU.mult)
                if fb < 3:
                    Wu = [next_slot(), next_slot()]
                    wu_off = O_WS + ((item[0] - 2) % 4) * 8192
                    wload(Wu[0], wview("w_up", l)[:, :, (fb + 1) * 1024:(fb + 1) * 1024 + 512])
                    wload(Wu[1], wview("w_up", l)[:, :, (fb + 1) * 1024 + 512:(fb + 2) * 1024])
                def prep_down(t, fb=fb, Wd=Wd):
                    b0 = 4 + 2 * (t % 2)
                    pd = ps[:, b0:b0 + 2, :].rearrange("p a n -> p (a n)")
                    for half in range(2):
                        for fc in range(8):
                            mm(pd[:, half * 512:(half + 1) * 512], UT[:, t, fc, :], Wd[half][:, fc, :],
                               fc == 0, fc == 7)
                    if fb == 0:
                        stt("dve", ACC[:, t, :], ACC[:, t, :], ALPHA, pd, ALU.mult, ALU.add)
                    else:
                        tt("dve", ACC[:, t, :], ACC[:, t, :], pd, ALU.add)
                    return ACC[:, t, :]

                if fb < 3:
                    for t in range(16):
                        prep_down(t)
                else:
                    G, B = load_gb(wu_off, "ln_ffn_g", "ln_ffn_b", l)
                    ln_pipeline(prep_down, G, B, l == nl - 1, hook=(l < nl - 1))

        if P.stopped:
            P.stopped = False
            offs = {"HI": O_HI, "LO": O_LO, "HT": O_HT, "A": O_A, "B": O_B, "M": O_M, "WS": O_WS}
            nby = min(32768, TOTAL - offs[dump])
            src = V(offs[dump], nby, BF16)
            P.dma("pool", out.rearrange("(p a) n -> p (a n)", a=16)[:, 0:nby // 2], src, is_output=True)
        P.emit()
    return nc


def _prep_inputs(inputs):
    f = lambda a: np.ascontiguousarray(np.asarray(a))
    shared = {}
    for k in ("w_in", "w_conv_out", "w_q_b", "w_kv_b", "w_mla_o", "w_out", "ln_mix_g", "ln_mix_b",
              "w_up", "w_down", "ln_ffn_g", "ln_ffn_b"):
        shared[k] = f(inputs[k]).astype(np.float32, copy=False)
    shared["ln_in_g"] = f(inputs["ln_in_g"]).reshape(1, D)
    shared["ln_in_b"] = f(inputs["ln_in_b"]).reshape(1, D)
    shared["b_gate"] = f(np.asarray(inputs["b_gate"]).reshape(NL, 16, 128).transpose(2, 0, 1))
    shared["conv_w"] = f(np.asarray(inputs["conv_w"]).reshape(NL, 3, 8, 128).transpose(3, 0, 1, 2))
    shared["q_norm_g"] = f(np.asarray(inputs["q_norm_g"]).reshape(NL, 6, 128).transpose(2, 0, 1))
    shared["kv_norm_g"] = f(np.asarray(inputs["kv_norm_g"]).reshape(NL, 2, 128).transpose(2, 0, 1))
    invf = (1.0 / (np.float32(10000.0) ** (np.arange(0, 32, 2, dtype=np.float32) / np.float32(32)))).astype(np.float32)
    shared["invf"] = f(np.broadcast_to(invf[None, :], (128, 16)))
    shared["ident"] = np.eye(128, dtype=np.float32)
    x = np.asarray(inputs["x"])
    pos = np.asarray(inputs["positions"]).astype(np.int32)
    maps = []
    for b in range(8):
        m = dict(shared)
        m["x"] = f(x[b])
        m["pos"] = f(pos[b].reshape(16, 128).T)
        maps.append(m)
    return maps


_NC_CACHE = {}


def kernel(**inputs):
    maps = _prep_inputs(inputs)
    if "nc" not in _NC_CACHE:
        _NC_CACHE["nc"] = build(NL)
    res = run_bass_kernel_spmd(_NC_CACHE["nc"], maps, core_ids=list(range(8)))
    return np.stack([np.asarray(r["out"], dtype=np.float32) for r in res.results], axis=0)
```

```python
import contextlib
import numpy as np
import concourse.bass as bass
import concourse.mybir as mybir
from concourse.bass_utils import run_bass_kernel_spmd

F32 = mybir.dt.float32
BF16 = mybir.dt.bfloat16
I32 = mybir.dt.int32
AF = mybir.ActivationFunctionType
ALU = mybir.AluOpType
DSIZE = {F32: 4, BF16: 2, I32: 4}

ENGS = ("pe", "act", "dve", "pool", "sp")
GRAN = 512
EPOCH = 20000
NDMASEM = 8

S = 2048
D = 1024
NL = 4
ALPHA = float((2 * NL) ** 0.25)
SCALE = float(96 ** -0.5)
LN_EPS = 1e-5
RMS_EPS = 1e-6
OFF_QA = 3072
OFF_KR = 4096
OFF_G = 4128


class Prog:
    def __init__(self, nc):
        self.nc = nc
        self.recs = []
        self.streams = {e: [] for e in ENGS}
        self.lastw = {}
        self.readers = {}
        self.dma_rr = {e: 0 for e in ENGS}
        self.dma_cnt = {}
        self.out_dma = []
        self.stopped = False

    def _grans(self, ap):
        t = ap.tensor
        if "DRam" in type(t).__name__:
            return ()
        dsz = DSIZE[ap.dtype]
        rowb = int(np.prod(list(t.shape)[1:])) * DSIZE[t.dtype]
        rowe = rowb // dsz
        col = int(ap.offset) % rowe
        ext = 1
        for s, c in ap.ap[1:]:
            ext += (c - 1) * abs(s)
        lo = col * dsz
        hi = (col + ext) * dsz
        key = t.name
        return [(key, g) for g in range(lo // GRAN, (hi - 1) // GRAN + 1)]

    def _deps(self, eng, reads, writes, is_dma):
        raw = set()
        war = set()
        rg = []
        wg = []
        for ap in reads:
            rg.extend(self._grans(ap))
        for ap in writes:
            wg.extend(self._grans(ap))
        for g in rg:
            w = self.lastw.get(g)
            if w is not None:
                raw.add(w)
        for g in wg:
            w = self.lastw.get(g)
            if w is not None:
                raw.add(w)
            for r in self.readers.get(g, ()):
                war.add(r)
        gid = len(self.recs)
        for g in rg:
            self.readers.setdefault(g, []).append(gid)
        for g in wg:
            self.lastw[g] = gid
            self.readers[g] = []
        deps = set()
        for d in raw | war:
            r = self.recs[d]
            if r["dma"]:
                deps.add(d)
                continue
            if r["eng"] == eng and not is_dma and eng == "pe":
                continue
            deps.add(d)
        for d in deps:
            self.recs[d]["flag"] = True
        return deps

    def op(self, eng, fn, reads=(), writes=()):
        if self.stopped:
            return -1
        deps = self._deps(eng, reads, writes, False)
        gid = len(self.recs)
        rec = dict(eng=eng, fn=fn, deps=deps, dma=False, flag=False, gid=gid)
        self.recs.append(rec)
        self.streams[eng].append(rec)
        return gid

    def dma(self, eng, out, in_, is_output=False, **kw):
        if self.stopped:
            return -1
        deps = self._deps(eng, [in_], [out], True)
        gid = len(self.recs)
        i = self.dma_rr[eng]
        self.dma_rr[eng] = (i + 1) % NDMASEM
        sname = f"{eng}_dma{i}"
        prev = self.dma_cnt.get(sname, 0)
        self.dma_cnt[sname] = prev + 16
        rec = dict(eng=eng, fn=lambda e: e.dma_start(out=out, in_=in_, **kw), deps=deps, dma=True,
                   flag=True, gid=gid, dsem=sname, dval=prev + 16, dprev=prev)
        self.recs.append(rec)
        self.streams[eng].append(rec)
        if is_output:
            self.out_dma.append(gid)
        return gid

    def emit(self):
        nc = self.nc
        for e in ENGS:
            k = 0
            for r in self.streams[e]:
                if r["flag"] and not r["dma"]:
                    k += 1
                    r["ord"] = k
        semnames = set()
        for e in ENGS:
            n = sum(1 for r in self.streams[e] if r["flag"] and not r["dma"])
            for ep in range(n // EPOCH + 1):
                semnames.add(f"{e}_c{ep}")
        for s in self.dma_cnt:
            semnames.add(s)
        semnames = sorted(semnames)
        with contextlib.ExitStack() as st:
            sems = {s: st.enter_context(nc.semaphore(s)) for s in semnames}
            block = st.enter_context(nc.Block())

            def token_wait(d):
                r = self.recs[d]
                if r["dma"]:
                    return (r["dsem"], r["dval"], None)
                o = r["ord"] - 1
                return (f"{r['eng']}_c{o // EPOCH}", o % EPOCH + 1, (r["eng"], o))

            def run_stream(e, engine):
                waited_eng = {}
                waited_dma = {}
                for r in self.streams[e]:
                    best = {}
                    for d in r["deps"]:
                        sname, val, eo = token_wait(d)
                        if eo is None:
                            if waited_dma.get(sname, 0) >= val:
                                continue
                            waited_dma[sname] = val
                        else:
                            pe_, o = eo
                            if waited_eng.get(pe_, -1) >= o:
                                continue
                            waited_eng[pe_] = o
                        best[sname] = max(best.get(sname, 0), val)
                    if r["dma"] and r["dprev"] > 0:
                        if waited_dma.get(r["dsem"], 0) < r["dprev"]:
                            waited_dma[r["dsem"]] = r["dprev"]
                            best[r["dsem"]] = max(best.get(r["dsem"], 0), r["dprev"])
                    for sname, val in best.items():
                        engine.wait_ge(sems[sname], val)
                    inst = r["fn"](engine)
                    if r["dma"]:
                        inst.then_inc(sems[r["dsem"]], 16)
                    elif r["flag"]:
                        o = r["ord"] - 1
                        inst.then_inc(sems[f"{e}_c{o // EPOCH}"], 1)
                if e == "sp":
                    for d in self.out_dma:
                        rr = self.recs[d]
                        engine.wait_ge(sems[rr["dsem"]], rr["dval"])

            @block.tensor
            def _(eng):
                run_stream("pe", eng)

            @block.scalar
            def _(eng):
                run_stream("act", eng)

            @block.vector
            def _(eng):
                run_stream("dve", eng)

            @block.gpsimd
            def _(eng):
                run_stream("pool", eng)

            @block.sync
            def _(eng):
                run_stream("sp", eng)


class Cut(Exception):
    pass


def build(nl=NL, cut=None, dump=None):
    nc = bass.Bass("TRN2", target_bir_lowering=False)
    Dm = {}

    def din(name, shape, dt=F32):
        Dm[name] = nc.dram_tensor(name, shape, dt, kind="ExternalInput").ap()

    din("x", [S, D])
    din("pos", [128, 16], I32)
    din("invf", [128, 16])
    din("ident", [128, 128])
    din("ln_in_g", [1, D])
    din("ln_in_b", [1, D])
    din("w_in", [NL, D, 6176])
    din("b_gate", [128, NL, 16])
    din("conv_w", [128, NL, 3, 8])
    din("w_conv_out", [NL, D, D])
    din("q_norm_g", [128, NL, 6])
    din("w_q_b", [NL, 768, 1536])
    din("kv_norm_g", [128, NL, 2])
    din("w_kv_b", [NL, 256, 2048])
    din("w_mla_o", [NL, D, D])
    din("w_out", [NL, D, D])
    din("ln_mix_g", [NL, D])
    din("ln_mix_b", [NL, D])
    din("w_up", [NL, D, 4 * D])
    din("w_down", [NL, 4 * D, D])
    din("ln_ffn_g", [NL, D])
    din("ln_ffn_b", [NL, D])
    out = nc.dram_tensor("out", [S, D], F32, kind="ExternalOutput").ap()

    TOTAL = 211968
    with contextlib.ExitStack() as st:
        ar = st.enter_context(nc.sbuf_tensor("arena", [128, TOTAL // 2], BF16))
        ps = st.enter_context(nc.psum_tensor("ps", [128, 8, 512], F32))
        P = Prog(nc)

        def ck(name):
            if cut == name:
                P.stopped = True

        def V(off, nbytes, dt=BF16, pat=None, **kw):
            assert off % 4 == 0
            a = ar[:, off // 2:(off + nbytes) // 2]
            if dt != BF16:
                a = a.bitcast(dt)
            if pat is not None:
                a = a.rearrange(pat, **kw)
            return a

        O_HI, O_LO, O_HT, O_A, O_B, O_WS, O_M = 0, 32768, 65536, 98304, 131072, 163840, 196608
        HI = V(O_HI, 32768, BF16, "p (a n) -> p a n", a=16)
        LO = V(O_LO, 32768, BF16, "p (a n) -> p a n", a=16)
        HT = V(O_HT, 32768, BF16, "p (a n) -> p a n", a=8)
        WS = [V(O_WS + i * 8192, 8192, BF16, "p (a n) -> p a n", a=8) for i in range(4)]
        IDB = V(O_M + 0, 256)
        ONESB = V(O_M + 512, 256)
        ONES32 = V(O_M + 1024, 512, F32)
        COS = V(O_M + 1536, 1024, F32, "p (a n) -> p a n", a=16)
        SIN = V(O_M + 2560, 1024, F32, "p (a n) -> p a n", a=16)
        COSQ = V(O_M + 3584, 1024, F32, "p (a n) -> p a n", a=16)
        SINQ = V(O_M + 4608, 1024, F32, "p (a n) -> p a n", a=16)
        BGATE = V(O_M + 5632, 256, F32, "p (a n) -> p a n", a=NL)
        CONVW = V(O_M + 5888, 384, F32, "p (l k c) -> p l k c", l=NL, k=3)
        QG = V(O_M + 6272, 96, F32, "p (a n) -> p a n", a=NL)
        KVG = V(O_M + 6368, 32, F32, "p (a n) -> p a n", a=NL)
        O_ST = O_M + 6656
        O_SC = O_M + 7680
        assert O_SC + 7680 == TOTAL

        def bank(b, n=512):
            return ps[:, b, 0:n]

        def bankbf(b):
            return ps[:, b, :].bitcast(BF16)

        def mm(out, lhsT, rhs, start, stop):
            P.op("pe", lambda e: e.matmul(out, lhsT=lhsT, rhs=rhs, start=start, stop=stop),
                 reads=[lhsT, rhs], writes=[out])

        def tr(out, in_):
            np_ = in_.shape[0]
            idn = IDB[0:np_, 0:np_]
            P.op("pe", lambda e: e.transpose(out=out, in_=in_, identity=idn), reads=[in_, idn], writes=[out])

        def tt(eng, out, in0, in1, op):
            P.op(eng, lambda e: e.tensor_tensor(out=out, in0=in0, in1=in1, op=op), reads=[in0, in1], writes=[out])

        def ts(eng, out, in0, s1, s2, op0, op1=None):
            rd = [in0] + [s for s in (s1, s2) if not isinstance(s, (int, float, type(None)))]
            if op1 is None:
                P.op(eng, lambda e: e.tensor_single_scalar(out=out, in_=in0, scalar=s1, op=op0), reads=rd, writes=[out])
            else:
                P.op(eng, lambda e: e.tensor_scalar(out=out, in0=in0, scalar1=s1, scalar2=s2, op0=op0, op1=op1),
                     reads=rd, writes=[out])

        def stt(eng, out, in0, sc, in1, op0, op1):
            rd = [in0, in1] + ([sc] if not isinstance(sc, (int, float)) else [])
            P.op(eng, lambda e: e.scalar_tensor_tensor(out=out, in0=in0, scalar=sc, in1=in1, op0=op0, op1=op1),
                 reads=rd, writes=[out])

        def cp(eng, out, in_):
            if eng == "act":
                P.op("act", lambda e: e.copy(out=out, in_=in_), reads=[in_], writes=[out])
            else:
                P.op(eng, lambda e: e.tensor_copy(out=out, in_=in_), reads=[in_], writes=[out])

        def act(out, in_, func, bias=None, scale=None):
            kw = {}
            rd = [in_]
            if bias is not None:
                kw["bias"] = bias
                if not isinstance(bias, (int, float)):
                    rd.append(bias)
            if scale is not None:
                kw["scale"] = scale
            P.op("act", lambda e: e.activation(out=out, in_=in_, func=func, **kw), reads=rd, writes=[out])

        def memset(eng, ap, val):
            P.op(eng, lambda e: e.memset(ap, val), reads=[], writes=[ap])

        evac_rr = [0]

        evac_mode = [None]

        def evac(out, in_):
            if evac_mode[0] is not None:
                cp(evac_mode[0], out, in_)
                return
            evac_rr[0] ^= 1
            cp("act" if evac_rr[0] else "dve", out, in_)

        def wview(name, l):
            return Dm[name][l].rearrange("(kc p) n -> p kc n", p=128)

        def wload(slot, src):
            P.dma("pool", slot, src)

        tmp32 = V(O_SC, 512, F32)
        P.dma("sp", tmp32, Dm["ident"])
        cp("dve", IDB, tmp32)
        memset("dve", ONESB, 1.0)
        memset("dve", ONES32, 1.0)
        P.dma("sp", BGATE, Dm["b_gate"])
        P.dma("sp", CONVW, Dm["conv_w"])
        P.dma("sp", QG, Dm["q_norm_g"])
        P.dma("sp", KVG, Dm["kv_norm_g"])
        posi = V(O_SC + 512, 64, I32)
        posf = V(O_SC + 1024, 64, F32)
        invf = V(O_SC + 1536, 64, F32)
        ang = V(O_SC + 2048, 1024, F32, "p (a n) -> p a n", a=16)
        arg = V(O_SC + 3072, 1024, F32, "p (a n) -> p a n", a=16)
        P.dma("sp", posi, Dm["pos"])
        P.dma("sp", invf, Dm["invf"])
        cp("dve", posf, posi)
        for t in range(16):
            ts("dve", ang[:, t, :], invf, posf[:, t:t + 1], None, ALU.mult)
        PI = float(np.pi)
        kf = V(O_SC + 4096, 1024, F32, "p (a n) -> p a n", a=16)
        ki = V(O_SC + 5120, 1024, I32, "p (a n) -> p a n", a=16)
        msk = V(O_SC + 6144, 1024, F32, "p (a n) -> p a n", a=16)
        for tab, shift in ((SIN, 0.0), (COS, 0.5 * PI)):
            ts("dve", arg, ang, shift, None, ALU.add)
            ts("dve", kf, arg, 1.0 / (2 * PI), None, ALU.mult)
            cp("dve", ki, kf)
            cp("dve", kf, ki)
            stt("dve", arg, kf, -2 * PI, arg, ALU.mult, ALU.add)
            ts("dve", msk, arg, PI, None, ALU.is_gt)
            stt("dve", arg, msk, -2 * PI, arg, ALU.mult, ALU.add)
            ts("dve", msk, arg, -PI, None, ALU.is_lt)
            stt("dve", arg, msk, 2 * PI, arg, ALU.mult, ALU.add)
            ts("dve", arg, arg, -PI, PI, ALU.max, ALU.min)
            act(tab, arg, AF.Sin)
        ts("dve", COSQ, COS, SCALE, None, ALU.mult)
        ts("dve", SINQ, SIN, SCALE, None, ALU.mult)

        ck("setup")

        def bc_heads(tab, t0):
            v = tab[:, t0:t0 + 2, :]
            a = [list(x) for x in v.ap]
            return bass.AP(v.tensor, v.offset, [a[0], a[1], [0, 2], a[2]])

        def ln_stage1(R, t, eps=LN_EPS):
            so = O_ST + (t % 2) * 512
            bn = V(so, 48, F32, "p (a n) -> p a n", a=2)
            mv = V(so + 64, 8, F32)
            rstd = V(so + 96, 4, F32)
            P.op("dve", lambda e: e.bn_stats(out=bn[:, 0, :], in_=R[:, 0:512]), reads=[R[:, 0:512]], writes=[bn[:, 0, :]])
            P.op("dve", lambda e: e.bn_stats(out=bn[:, 1, :], in_=R[:, 512:1024]), reads=[R[:, 512:1024]], writes=[bn[:, 1, :]])
            P.op("dve", lambda e: e.bn_aggr(out=mv, in_=bn), reads=[bn], writes=[mv])
            ts("dve", rstd, mv[:, 1:2], eps, None, ALU.add)
            act(rstd, rstd, AF.Sqrt)

        def ln_stage2(Rin, R, G, B, t, final):
            so = O_ST + (t % 2) * 512
            mv = V(so + 64, 8, F32)
            rstd = V(so + 96, 4, F32)
            P.op("dve", lambda e: e.reciprocal(out=rstd, in_=rstd), reads=[rstd], writes=[rstd])
            stt("dve", R, Rin, mv[:, 0:1], G, ALU.subtract, ALU.mult)
            stt("dve", R, R, rstd, B, ALU.mult, ALU.add)
            if final:
                P.dma("sp", out[t * 128:(t + 1) * 128, :], R, is_output=True)
            else:
                cp("act", HI[:, t, :], R)
                tt("pool", LO[:, t, :], R, HI[:, t, :], ALU.subtract)

        def ln_pipeline(prep, G, B, final, eps=LN_EPS, hook=True):
            Rs = {}
            for t in range(17):
                if t < 16:
                    r = prep(t)
                    Rs[t] = r if isinstance(r, tuple) else (r, r)
                    ln_stage1(Rs[t][0], t, eps)
                if t >= 1:
                    ln_stage2(Rs[t - 1][0], Rs[t - 1][1], G, B, t - 1, final)
                    if hook:
                        ln_loop_hook(t - 1)

        def load_gb(off, gname, bname, l):
            G = V(off, 4096, F32)
            B = V(off + 4096, 4096, F32)
            gs = Dm[gname][l:l + 1, :] if l is not None else Dm[gname]
            bs = Dm[bname][l:l + 1, :] if l is not None else Dm[bname]
            P.dma("sp", G.unsqueeze(1), gs.partition_broadcast(128))
            P.dma("sp", B.unsqueeze(1), bs.partition_broadcast(128))
            return G, B

        hb = [0]

        hT_banks = [(0, 1, 2, 3)]

        def build_hT_block(tq):
            for kc in range(8):
                bl = hT_banks[0]
                b = bl[hb[0] % len(bl)]
                hb[0] += 1
                pt = bankbf(b)[:, 0:512].rearrange("p (a n) -> p a n", a=4)
                for j in range(4):
                    tr(pt[:, j, :], HI[:, tq * 4 + j, kc * 128:(kc + 1) * 128])
                evac(HT[:, kc, tq * 512:(tq + 1) * 512], bankbf(b)[:, 0:512])

        def build_hT():
            for tq in range(4):
                build_hT_block(tq)

        def ln_loop_hook(t):
            evac_mode[0] = "act"
            if t >= 6 and (t - 3) % 4 == 3:
                build_hT_block((t - 3) // 4)
            if t == 15:
                build_hT_block(3)
            evac_mode[0] = None

        G0, B0 = load_gb(O_B, "ln_in_g", "ln_in_b", None)
        def prep_entry(t):
            X = V(O_B + 8192 + (t % 4) * 4096, 4096, F32)
            P.dma("sp", X, Dm["x"][t * 128:(t + 1) * 128, :])
            return X

        ln_pipeline(prep_entry, G0, B0, False)
        ck("entryln")
        item = [0]

        def next_slot():
            s = WS[item[0] % 4]
            item[0] += 1
            return s

        for l in range(nl):
            w_in = wview("w_in", l)
            ck("hT")
            base = item[0] % 4
            item[0] += 4 + (0 if l == 0 else 2)
            sh = 0 if l == 0 else 2
            base_slot = (base + sh) % 4
            S1 = WS[(base + sh) % 4]; S2 = WS[(base + sh + 1) % 4]
            A0 = WS[(base + sh + 2) % 4]; Bg0 = WS[(base + sh + 3) % 4]
            wload(S1, w_in[:, :, OFF_QA:OFF_QA + 512])
            wload(S2, w_in[:, :, OFF_QA + 512:OFF_QA + 1024])
            KRW = V(O_SC, 512, BF16, "p (a n) -> p a n", a=8)
            wload(KRW, w_in[:, :, OFF_KR:OFF_KR + 32])
            wload(A0, wview("w_mla_o", l)[:, :, 0:512])
            wload(Bg0, w_in[:, :, OFF_G + 1024:OFF_G + 1536])
            ck("wl")
            QKVN = V(O_A, 32768, BF16, "p (a n) -> p a n", a=8)
            RAW = V(O_B, 16384, F32, "p (a n) -> p a n", a=8)
            SQ = V(O_B + 16384, 8192, BF16, "p (a n) -> p a n", a=8)
            RSTD = V(O_B + 24576, 4096, F32, "p (a n) -> p a n", a=2)
            KRTM = V(O_B + 28672, 3072, BF16, "p (a n) -> p a n", a=16)
            memset("dve", KRTM, 0.0)
            bi = 0
            for tq in range(4):
                tsl = slice(tq * 512, (tq + 1) * 512)
                for j in range(8):
                    b = bi % 4
                    bi += 1
                    Sx = S1 if j < 4 else S2
                    jl = j % 4
                    for kc in range(8):
                        mm(bank(b), Sx[:, kc, jl * 128:(jl + 1) * 128], HT[:, kc, tsl], kc == 0, kc == 7)
                    ck("p1m")
                    cp("act", RAW[:, j, :], bank(b))
                    ck("p1c")
                    tt("dve", SQ[:, j, :], RAW[:, j, :], RAW[:, j, :], ALU.mult)
                ck("p1j")
                for gi, js, nf in ((0, list(range(6)), 768.0), (1, [6, 7], 256.0)):
                    pb = bank(4 + gi)
                    for idx, j in enumerate(js):
                        mm(pb, ONESB, SQ[:, j, :], idx == 0, idx == len(js) - 1)
                    ts("dve", RSTD[:, gi, :], pb, 1.0 / nf, RMS_EPS, ALU.mult, ALU.add)
                    act(RSTD[:, gi, :], RSTD[:, gi, :], AF.Ln)
                    act(RSTD[:, gi, :], RSTD[:, gi, :], AF.Exp, scale=-0.5)
                ck("p1s")
                for j in range(8):
                    gi = 0 if j < 6 else 1
                    gcol = QG[:, l, j:j + 1] if j < 6 else KVG[:, l, j - 6:j - 5]
                    stt("dve", QKVN[:, j, tsl], RAW[:, j, :], gcol, RSTD[:, gi, :], ALU.mult, ALU.mult)
                ck("p1n")
                pk = ps[:, 6, 0:128].rearrange("p (a n) -> p a n", a=4)
                for j in range(4):
                    t = tq * 4 + j
                    for kc in range(8):
                        mm(pk[:, j, :], HT[:, kc, t * 128:(t + 1) * 128], KRW[:, kc, :], kc == 0, kc == 7)
                ck("p1k")
                x1 = pk[:, :, 0:16]
                x2 = pk[:, :, 16:32]
                c_ = COS[:, tq * 4:tq * 4 + 4, :]
                s_ = SIN[:, tq * 4:tq * 4 + 4, :]
                T = [V(O_SC + 1024 + i * 512, 256, F32, "p (a n) -> p a n", a=4) for i in range(4)]
                tsl4 = slice(tq * 4, tq * 4 + 4)
                tt("dve", T[0], x1, c_, ALU.mult)
                tt("dve", T[1], x2, s_, ALU.mult)
                tt("dve", KRTM[:, tsl4, 64:80], T[0], T[1], ALU.subtract)
                tt("dve", T[2], x1, s_, ALU.mult)
                tt("dve", T[3], x2, c_, ALU.mult)
                tt("dve", KRTM[:, tsl4, 80:96], T[2], T[3], ALU.add)

            ck("p1")
            AT = V(O_B, 32768, BF16, "p (a n) -> p a n", a=8)
            o_t2 = O_WS + base_slot * 8192
            QT = [V(O_HT + i * 8192, 8192, BF16, "p (a n) -> p a n", a=2) for i in range(2)]
            KT = [V(O_HT + 16384 + i * 4096, 4096, BF16) for i in range(2)]
            PT = [V(O_HT + 24576 + i * 1024, 1024, BF16) for i in range(3)]
            OS = V(O_HT + 27648, 2048, F32)
            QTM = [V(O_HT + 29696 + i * 768, 768, BF16, "p (a h d) -> p a h d", a=2, h=2) for i in range(2)]
            RT = [V(O_HT + 31232 + i * 256, 256, F32, "p (a h d) -> p a h d", a=2, h=2) for i in range(4)]
            RB = V(o_t2 + 12288, 1024, BF16)
            LNR = V(o_t2 + 13312, 2048, F32)
            VP = [V(o_t2 + i * 6144, 6144, BF16, "p (a n) -> p a n", a=16) for i in range(2)]
            WQb = [V(O_SC + i * 3584, 2304, BF16, "p (a n) -> p a n", a=6) for i in range(2)]
            WKVb = [V(O_SC + i * 3584 + 2560, 1024, BF16, "p (a n) -> p a n", a=2) for i in range(2)]
            for tq in range(4):
                ptb = bankbf(7)[:, 0:512]
                pt4 = ptb.rearrange("p (a n) -> p a n", a=4)
                for j in range(4):
                    tr(pt4[0:96, j, :], KRTM[:, tq * 4 + j, :])
                cp("dve", KT[0][64:96, tq * 512:(tq + 1) * 512], ptb[64:96, :])
                cp("dve", KT[1][64:96, tq * 512:(tq + 1) * 512], ptb[64:96, :])
            for i in range(2):
                memset("pool", VP[i][:, :, 64:128], 0.0)
                memset("pool", VP[i][:, :, 64:65], 1.0)

            wq_all = Dm["w_q_b"][l].rearrange("(kc p) n -> p kc n", p=128)
            wkv_all = Dm["w_kv_b"][l].rearrange("(kc p) n -> p kc n", p=128)

            prb = [0]

            def prod_bank():
                prb[0] ^= 1
                return 6 + prb[0]

            def load_pair_w(hp):
                wload(WQb[hp % 2], wq_all[:, :, hp * 192:(hp + 1) * 192])
                wload(WKVb[hp % 2], wkv_all[:, :, hp * 256:(hp + 1) * 256])

            def prod_qv(hp):
                par = hp % 2
                WQ, WKV = WQb[par], WKVb[par]
                yield
                for tq in range(4):
                    pT = bankbf(5).rearrange("p (h n) -> p h n", h=2)
                    for half in range(2):
                        t0 = tq * 4 + half * 2
                        bq = prod_bank()
                        for a in range(2):
                            for kc in range(6):
                                mm(ps[:, bq, a * 192:(a + 1) * 192], QKVN[:, kc, (t0 + a) * 128:(t0 + a + 1) * 128],
                                   WQ[:, kc, :], kc == 0, kc == 5)
                        yield
                        pq = ps[:, bq, 0:384].rearrange("p (a h d) -> p a h d", a=2, h=2)
                        qm = QTM[half]
                        ts("dve", qm[:, :, :, 0:64], pq[:, :, :, 0:64], SCALE, None, ALU.mult)
                        x1 = pq[:, :, :, 64:80]
                        x2 = pq[:, :, :, 80:96]
                        cq = bc_heads(COSQ, t0)
                        sq_ = bc_heads(SINQ, t0)
                        tt("dve", RT[0], x1, cq, ALU.mult)
                        tt("dve", RT[1], x2, sq_, ALU.mult)
                        tt("dve", qm[:, :, :, 64:80], RT[0], RT[1], ALU.subtract)
                        tt("dve", RT[2], x1, sq_, ALU.mult)
                        tt("dve", RT[3], x2, cq, ALU.mult)
                        tt("dve", qm[:, :, :, 80:96], RT[2], RT[3], ALU.add)
                        for _ in range(7):
                            yield
                        for a in range(2):
                            for h in range(2):
                                c0 = (half * 2 + a) * 128
                                tr(pT[0:96, h, c0:c0 + 128], qm[:, a, h, :])
                        yield
                    yield
                    cp("dve", QT[par][0:96, :, tq * 512:(tq + 1) * 512], pT[0:96, :, :])
                    yield
                for t4 in range(4):
                    bv = prod_bank()
                    pv = ps[:, bv, :].rearrange("p (a h d) -> p a h d", a=4, h=2)
                    for a in range(4):
                        t = t4 * 4 + a
                        for kc in range(2):
                            rv = WKV[:, kc, :].rearrange("p (h t d) -> p h t d", h=2, t=2)[:, :, 1, :]
                            mm(pv[:, a, :, :], QKVN[:, 6 + kc, t * 128:(t + 1) * 128], rv, kc == 0, kc == 1)
                    yield
                    yield
                    cp("dve", VP[par][:, t4 * 4:t4 * 4 + 4, 0:64], pv[:, :, 0, :])
                    cp("dve", VP[par][:, t4 * 4:t4 * 4 + 4, 128:192], pv[:, :, 1, :])
                    yield

            def prod_k(n):
                hp, hl = n // 2, n % 2
                WKV = WKVb[hp % 2]
                for tq in range(4):
                    bk = prod_bank()
                    pk_ = ps[0:64, bk, :]
                    for kc in range(2):
                        mm(pk_, WKV[:, kc, hl * 128:hl * 128 + 64], QKVN[:, 6 + kc, tq * 512:(tq + 1) * 512],
                           kc == 0, kc == 1)
                    yield
                    yield
                    cp("dve", KT[hl][0:64, tq * 512:(tq + 1) * 512], pk_)
                    yield

            def drain(g):
                if g is not None:
                    for _ in g:
                        pass

            load_pair_w(0)
            load_pair_w(1)
            drain(prod_qv(0))
            drain(prod_k(0))

            steps = [(n, tq, kt) for n in range(16) for tq in range(4) for kt in range(16)]
            NS = len(steps)

            def rec_S(si):
                n, tq, kt = steps[si]
                hp, hl = n // 2, n % 2
                mm(bank(si % 3), KT[hl][0:96, kt * 128:(kt + 1) * 128], QT[hp % 2][0:96, hl, tq * 512:(tq + 1) * 512],
                   True, True)

            gen_qv = None
            gen_k = None
            deferred = []
            rec_S(0)
            rec_S(1)
            oi = 0
            for si in range(NS):
                n, tq, kt = steps[si]
                hp, hl = n // 2, n % 2
                if tq == 0 and kt == 0:
                    drain(gen_k)
                    gen_k = prod_k(n + 1) if n + 1 < 16 else None
                    if hl == 0:
                        drain(gen_qv)
                        gen_qv = prod_qv(hp + 1) if hp + 1 < 8 else None
                    elif hp + 2 < 8:
                        load_pair_w(hp + 2)
                pt_ = PT[si % 3]
                act(pt_, bank(si % 3), AF.Exp)
                if si + 2 < NS:
                    n2 = steps[si + 2][0]
                    if n2 != n:
                        drain(gen_k); gen_k = None
                        if n2 % 2 == 0:
                            drain(gen_qv); gen_qv = None
                    rec_S(si + 2)
                po = ps[:, 3 + (oi % 2), :]
                if hl == 0:
                    mm(po[0:65, :], VP[hp % 2][:, kt, 0:65], pt_, kt == 0, kt == 15)
                else:
                    mm(po[:, :], VP[hp % 2][:, kt, 64:192], pt_, kt == 0, kt == 15)
                for dd in deferred:
                    dd[0] -= 1
                for dd in [d_ for d_ in deferred if d_[0] <= 0]:
                    deferred.remove(dd)
                    dd[1]()
                if kt == 15:
                    tsl = slice(tq * 512, (tq + 1) * 512)
                    if hl == 0:
                        def part1(po=po):
                            act(LNR[64:65, :], po[64:65, :], AF.Ln)
                            act(RB[64:65, :], LNR[64:65, :], AF.Exp, scale=-1.0)
                            cp("dve", OS[0:64, :], po[0:64, :])

                        def part2(hp=hp, tsl=tsl, pbc=po):
                            mm(pbc[0:64, :], ONESB[64:65, 0:64], RB[64:65, :], True, True)
                            tt("dve", AT[0:64, hp, tsl], OS[0:64, :], pbc[0:64, :], ALU.mult)
                    else:
                        def part1(po=po):
                            act(LNR[0:1, :], po[0:1, :], AF.Ln)
                            act(RB[0:1, :], LNR[0:1, :], AF.Exp, scale=-1.0)
                            cp("dve", OS[64:128, :], po[64:128, :])

                        def part2(hp=hp, tsl=tsl, pbc=po):
                            mm(pbc[:, :], ONESB[0:1, :], RB[0:1, :], True, True)
                            tt("dve", AT[64:128, hp, tsl], OS[64:128, :], pbc[64:128, :], ALU.mult)
                    deferred.append([2, part1])
                    deferred.append([8, part2])
                    oi += 1
                for g in (gen_qv, gen_k):
                    if g is not None:
                        next(g, None)
            while deferred:
                deferred.pop(0)[1]()

            ck("p2")
            MT = V(O_A, 32768, BF16, "p (a n) -> p a n", a=8)
            GT = [V(O_SC + i * 2048, 2048, F32) for i in range(2)]
            build_hT_block(0)
            A1 = next_slot(); wload(A1, wview("w_mla_o", l)[:, :, 512:1024])
            Bg1 = next_slot(); wload(Bg1, w_in[:, :, OFF_G + 1536:OFF_G + 2048])
            gi_ = 0
            Wc = [None, None]; Wh = [None, None]; Wb = [None, None]
            for cp_ in range(2):
                A_, B_ = (A0, Bg0) if cp_ == 0 else (A1, Bg1)
                if cp_ == 1:
                    Wc[0] = next_slot(); wload(Wc[0], w_in[:, :, 1024:1536])
                    Wh[0] = next_slot(); wload(Wh[0], w_in[:, :, 2048:2560])
                order3 = [(c, tq) for tq in range(4) for c in range(4)] if cp_ == 0 else \
                         [(c, tq) for c in range(4) for tq in range(4)]
                for c, tq in order3:
                    if cp_ == 0 and c == 0 and tq + 1 < 4:
                        build_hT_block(tq + 1)
                    cabs = cp_ * 4 + c
                    tsl = slice(tq * 512, (tq + 1) * 512)
                    py = bank(4 + gi_ % 2)
                    pg = bank(6 + gi_ % 2)
                    g_ = GT[gi_ % 2]
                    gi_ += 1
                    for kc in range(8):
                        mm(py, A_[:, kc, c * 128:(c + 1) * 128], AT[:, kc, tsl], kc == 0, kc == 7)
                    for kc in range(8):
                        mm(pg, B_[:, kc, c * 128:(c + 1) * 128], HT[:, kc, tsl], kc == 0, kc == 7)
                    act(g_, pg, AF.Sigmoid, bias=BGATE[:, l, 8 + cabs:9 + cabs])
                    stt("dve", MT[:, cabs, tsl], g_, 1.0 / ALPHA, py, ALU.mult, ALU.mult)

            ck("p3")
            ZT = V(O_B, 32768, BF16, "p (a n) -> p a n", a=8)
            U = V(O_SC, 4100, BF16)
            TC = V(O_SC + 4608, 2048, F32)
            memset("pool", U[:, 0:2], 0.0)
            memset("pool", U[:, 2048:2050], 0.0)
            Wb[0] = next_slot(); wload(Wb[0], w_in[:, :, 0:512])
            Wc[1] = next_slot(); wload(Wc[1], w_in[:, :, 1536:2048])
            gi_ = 0
            for sc in range(2):
                if sc == 1:
                    Wh[1] = next_slot(); wload(Wh[1], w_in[:, :, 2560:3072])
                    Wb[1] = next_slot(); wload(Wb[1], w_in[:, :, 512:1024])
                for c in range(4):
                    cabs = sc * 4 + c
                    csl = slice(c * 128, (c + 1) * 128)
                    for tq in range(4):
                        tsl = slice(tq * 512, (tq + 1) * 512)
                        pc = bank(gi_ % 2)
                        ph = bank(2 + gi_ % 2)
                        gi_ += 1
                        for kc in range(8):
                            mm(pc, Wc[sc][:, kc, csl], HT[:, kc, tsl], kc == 0, kc == 7)
                        for kc in range(8):
                            mm(ph, Wh[sc][:, kc, csl], HT[:, kc, tsl], kc == 0, kc == 7)
                        usl = U[:, 1 + tq * 512:1 + (tq + 1) * 512]
                        cp("act", usl, pc)
                        tt("dve", usl, usl, ph, ALU.mult)
                    for tq in range(4):
                        tsl = slice(tq * 512, (tq + 1) * 512)
                        pb = bank(4 + tq % 2)
                        for kc in range(8):
                            mm(pb, Wb[sc][:, kc, csl], HT[:, kc, tsl], kc == 0, kc == 7)
                        w0 = CONVW[:, l, 0, cabs:cabs + 1]
                        w1 = CONVW[:, l, 1, cabs:cabs + 1]
                        w2 = CONVW[:, l, 2, cabs:cabs + 1]
                        ts("dve", TC, U[:, tq * 512:tq * 512 + 512], w0, None, ALU.mult)
                        stt("dve", TC, U[:, tq * 512 + 1:tq * 512 + 513], w1, TC, ALU.mult, ALU.add)
                        stt("dve", TC, U[:, tq * 512 + 2:tq * 512 + 514], w2, TC, ALU.mult, ALU.add)
                        tt("dve", ZT[:, cabs, tsl], TC, pb, ALU.mult)

            ck("p4a")
            GT = [V(O_SC + i * 2048, 2048, F32) for i in range(2)]
            T2 = V(O_SC + 4096, 2048, F32)
            Wco = [None, None]; Wg = [None, None]
            Wco[0] = next_slot(); wload(Wco[0], wview("w_conv_out", l)[:, :, 0:512])
            Wg[0] = next_slot(); wload(Wg[0], w_in[:, :, OFF_G:OFF_G + 512])
            gi_ = 0
            for cp_ in range(2):
                if cp_ == 1:
                    Wco[1] = next_slot(); wload(Wco[1], wview("w_conv_out", l)[:, :, 512:1024])
                    Wg[1] = next_slot(); wload(Wg[1], w_in[:, :, OFF_G + 512:OFF_G + 1024])
                for c in range(4):
                    cabs = cp_ * 4 + c
                    csl = slice(c * 128, (c + 1) * 128)
                    for tq in range(4):
                        tsl = slice(tq * 512, (tq + 1) * 512)
                        py = bank(gi_ % 2)
                        pg = bank(2 + gi_ % 2)
                        g_ = GT[gi_ % 2]
                        gi_ += 1
                        for kc in range(8):
                            mm(pg, Wg[cp_][:, kc, csl], HT[:, kc, tsl], kc == 0, kc == 7)
                        for kc in range(8):
                            mm(py, Wco[cp_][:, kc, csl], ZT[:, kc, tsl], kc == 0, kc == 7)
                        act(g_, pg, AF.Sigmoid, bias=BGATE[:, l, cabs:cabs + 1])
                        stt("dve", T2, g_, 1.0 / ALPHA, py, ALU.mult, ALU.mult)
                        tt("dve", MT[:, cabs, tsl], T2, MT[:, cabs, tsl], ALU.add)

            ck("p4b")
            Wo = [next_slot(), next_slot()]
            wload(Wo[0], wview("w_out", l)[:, :, 0:512])
            wload(Wo[1], wview("w_out", l)[:, :, 512:1024])
            Wu = [next_slot(), next_slot()]
            wload(Wu[0], wview("w_up", l)[:, :, 0:512])
            wload(Wu[1], wview("w_up", l)[:, :, 512:1024])
            G, B = load_gb(O_B, "ln_mix_g", "ln_mix_b", l)
            def prep_mix(t):
                b0 = 2 + 2 * (t % 3)
                pm = ps[:, b0:b0 + 2, :].rearrange("p a n -> p (a n)")
                R = V(O_B + 8192 + (t % 4) * 4096, 4096, F32)
                for half in range(2):
                    hs = slice(half * 512, (half + 1) * 512)
                    for kc in range(8):
                        mm(pm[:, hs], MT[:, kc, t * 128:(t + 1) * 128], Wo[half][:, kc, :], kc == 0, False)
                    mm(pm[:, hs], IDB, HI[:, t, hs], False, False)
                    mm(pm[:, hs], IDB, LO[:, t, hs], False, True)
                return (pm, R)

            hT_banks[0] = (0, 1)
            ln_pipeline(prep_mix, G, B, False, eps=LN_EPS / (ALPHA * ALPHA))
            hT_banks[0] = (0, 1, 2, 3)

            ck("p5")
            ACC = V(O_A, 65536, F32, "p (a n) -> p a n", a=16)
            for t in range(16):
                tt("dve", ACC[:, t, :], HI[:, t, :], LO[:, t, :], ALU.add)
            UT = V(O_HI, 32768, BF16, "p (t f n) -> p t f n", t=16, f=8)
            TM = [V(O_SC + i * 2048, 2048, F32) for i in range(2)]
            ui = 0
            wu_off = 0
            for fb in range(4):
                Wd = [next_slot(), next_slot()]
                wd_off = O_WS + ((item[0] - 2) % 4) * 8192
                wd_v = Dm["w_down"][l, fb * 1024:(fb + 1) * 1024, :].rearrange("(fc p) n -> p fc n", p=128)
                wload(Wd[0], wd_v[:, :, 0:512])
                wload(Wd[1], wd_v[:, :, 512:1024])
                order = [(fc, tq) for tq in range(4) for fc in range(8)] if fb == 0 else \
                        [(fc, tq) for fc in range(8) for tq in range(4)]
                for fc, tq in order:
                    Sx = Wu[fc // 4]
                    fl = fc % 4
                    tsl = slice(tq * 512, (tq + 1) * 512)
                    pu = bank(ui % 4)
                    tm = TM[ui % 2]
                    ui += 1
                    for kc in range(8):
                        mm(pu, Sx[:, kc, fl * 128:(fl + 1) * 128], HT[:, kc, tsl], kc == 0, kc == 7)
                    act(tm, pu, AF.Relu)
                    tm4 = tm.rearrange("p (a n) -> p a n", a=4)
                    tt("dve", UT[:, tq * 4:(tq + 1) * 4, fc, :], tm4, tm4, ALU.mult)
                if fb < 3:
                    Wu = [next_slot(), next_slot()]
                    wu_off = O_WS + ((item[0] - 2) % 4) * 8192
                    wload(Wu[0], wview("w_up", l)[:, :, (fb + 1) * 1024:(fb + 1) * 1024 + 512])
                    wload(Wu[1], wview("w_up", l)[:, :, (fb + 1) * 1024 + 512:(fb + 2) * 1024])
                def prep_down(t, fb=fb, Wd=Wd):
                    b0 = 4 + 2 * (t % 2)
                    pd = ps[:, b0:b0 + 2, :].rearrange("p a n -> p (a n)")
                    for half in range(2):
                        for fc in range(8):
                            mm(pd[:, half * 512:(half + 1) * 512], UT[:, t, fc, :], Wd[half][:, fc, :],
                               fc == 0, fc == 7)
                    if fb == 0:
                        stt("dve", ACC[:, t, :], ACC[:, t, :], ALPHA, pd, ALU.mult, ALU.add)
                    else:
                        tt("dve", ACC[:, t, :], ACC[:, t, :], pd, ALU.add)
                    return ACC[:, t, :]

                if fb < 3:
                    for t in range(16):
                        prep_down(t)
                else:
                    G, B = load_gb(wu_off, "ln_ffn_g", "ln_ffn_b", l)
                    ln_pipeline(prep_down, G, B, l == nl - 1, hook=(l < nl - 1))

        if P.stopped:
            P.stopped = False
            offs = {"HI": O_HI, "LO": O_LO, "HT": O_HT, "A": O_A, "B": O_B, "M": O_M, "WS": O_WS}
            nby = min(32768, TOTAL - offs[dump])
            src = V(offs[dump], nby, BF16)
            P.dma("pool", out.rearrange("(p a) n -> p (a n)", a=16)[:, 0:nby // 2], src, is_output=True)
        P.emit()
    return nc


def _prep_inputs(inputs):
    f = lambda a: np.ascontiguousarray(np.asarray(a))
    shared = {}
    for k in ("w_in", "w_conv_out", "w_q_b", "w_kv_b", "w_mla_o", "w_out", "ln_mix_g", "ln_mix_b",
              "w_up", "w_down", "ln_ffn_g", "ln_ffn_b"):
        shared[k] = f(inputs[k]).astype(np.float32, copy=False)
    shared["ln_in_g"] = f(inputs["ln_in_g"]).reshape(1, D)
    shared["ln_in_b"] = f(inputs["ln_in_b"]).reshape(1, D)
    shared["b_gate"] = f(np.asarray(inputs["b_gate"]).reshape(NL, 16, 128).transpose(2, 0, 1))
    shared["conv_w"] = f(np.asarray(inputs["conv_w"]).reshape(NL, 3, 8, 128).transpose(3, 0, 1, 2))
    shared["q_norm_g"] = f(np.asarray(inputs["q_norm_g"]).reshape(NL, 6, 128).transpose(2, 0, 1))
    shared["kv_norm_g"] = f(np.asarray(inputs["kv_norm_g"]).reshape(NL, 2, 128).transpose(2, 0, 1))
    invf = (1.0 / (np.float32(10000.0) ** (np.arange(0, 32, 2, dtype=np.float32) / np.float32(32)))).astype(np.float32)
    shared["invf"] = f(np.broadcast_to(invf[None, :], (128, 16)))
    shared["ident"] = np.eye(128, dtype=np.float32)
    x = np.asarray(inputs["x"])
    pos = np.asarray(inputs["positions"]).astype(np.int32)
    maps = []
    for b in range(8):
        m = dict(shared)
        m["x"] = f(x[b])
        m["pos"] = f(pos[b].reshape(16, 128).T)
        maps.append(m)
    return maps


_NC_CACHE = {}


def kernel(**inputs):
    maps = _prep_inputs(inputs)
    if "nc" not in _NC_CACHE:
        _NC_CACHE["nc"] = build(NL)
    res = run_bass_kernel_spmd(_NC_CACHE["nc"], maps, core_ids=list(range(8)))
    return np.stack([np.asarray(r["out"], dtype=np.float32) for r in res.results], axis=0)
```
